# Optimizing a Trainium2 kernel written in Bass

```python
import math
import jax, jax.numpy as jnp
from jax import lax
import numpy as np

D_MODEL = 1024
BATCH = 8
SEQ = 2048
DEPTH = 1
DEC_BATCH = 32
DEC_SEQ = 64
PAST_LEN = 2048

CHUNK = 64
N_HEADS = 16
HEAD_DIM = 64
N_KV_HEADS = 4
Q_PER_KV = N_HEADS // N_KV_HEADS
IDX_HEADS = 8
IDX_DIM = 64
TOPK_MAX = 256
QBLK = 128
ROPE_THETA = 10000.0
D_INNER = 2 * D_MODEL
SSM_HEAD_DIM = 64
SSM_HEADS = D_INNER // SSM_HEAD_DIM
SSM_GROUPS = 8
D_STATE = 128
CONV_W = 4
CONV_CH = D_INNER + 2 * SSM_GROUPS * D_STATE
SSD_CHUNK = CHUNK
D_FF = -(-8 * D_MODEL // (3 * 256)) * 256
EPS = 1e-6
IN_SIZES = (N_HEADS * HEAD_DIM, N_KV_HEADS * HEAD_DIM, N_KV_HEADS * HEAD_DIM, IDX_HEADS * IDX_DIM, IDX_DIM, IDX_HEADS, D_INNER, CONV_CH, SSM_HEADS, 2 * D_MODEL)
IN_DIM = sum(IN_SIZES)

kernel_name = 'dsa_ssd_hybrid_stream_step'


def rms_normalize(x):
    x32 = x.astype(jnp.float32)
    return (x32 * lax.rsqrt(jnp.mean(x32 * x32, axis=-1, keepdims=True) + EPS)).astype(x.dtype)


def rope(x, pos):
    half = x.shape[-1] // 2
    inv = ROPE_THETA ** (-jnp.arange(half, dtype=jnp.float32) / half)
    ang = pos.astype(jnp.float32)[:, None] * inv[None, :]
    cos = jnp.cos(ang)[:, None, :].astype(x.dtype)
    sin = jnp.sin(ang)[:, None, :].astype(x.dtype)
    x1, x2 = x[..., :half], x[..., half:]
    return jnp.concatenate([x1 * cos - x2 * sin, x2 * cos + x1 * sin], axis=-1)


def dsa_attention(q, qi, wi, qpos, k_all, v_all, ki_all):
    t = q.shape[1]
    n_keys = k_all.shape[1]
    topk = min(TOPK_MAX, n_keys // 4)
    qb = min(QBLK, t)
    nb = t // qb
    kpos = jnp.arange(n_keys)
    qpos_blocks = qpos.reshape(nb, qb)

    def one_seq(args):
        q_s, qi_s, wi_s, k_s, v_s, ki_s = args

        def one_block(bargs):
            qq, qqi, ww, pp = bargs
            logits = jnp.einsum('thd,sd->ths', qqi, ki_s).astype(jnp.float32) * (IDX_DIM ** -0.5)
            score = jnp.einsum('th,ths->ts', ww.astype(jnp.float32), jax.nn.relu(logits))
            limit = (pp // CHUNK + 1) * CHUNK
            score = jnp.where(kpos[None, :] < limit[:, None], score, -jnp.inf)
            _, idx = lax.top_k(score, topk)
            valid = idx < limit[:, None]
            ks = k_s[idx]
            vs = v_s[idx]
            qg = qq.reshape(qb, N_KV_HEADS, Q_PER_KV, HEAD_DIM)
            s = jnp.einsum('tkgd,tjkd->tkgj', qg, ks).astype(jnp.float32) * (HEAD_DIM ** -0.5)
            s = jnp.where(valid[:, None, None, :], s, -jnp.inf)
            p = jax.nn.softmax(s, axis=-1).astype(vs.dtype)
            return jnp.einsum('tkgj,tjkd->tkgd', p, vs).reshape(qb, N_HEADS * HEAD_DIM)

        out = lax.map(one_block, (q_s.reshape(nb, qb, N_HEADS, HEAD_DIM), qi_s.reshape(nb, qb, IDX_HEADS, IDX_DIM), wi_s.reshape(nb, qb, IDX_HEADS), qpos_blocks))
        return out.reshape(t, N_HEADS * HEAD_DIM)

    return lax.map(one_seq, (q, qi, wi, k_all, v_all, ki_all))


def causal_conv(full, w_conv, b_conv, t):
    out = b_conv
    for j in range(CONV_W):
        out = out + full[:, j:j + t] * w_conv[j]
    return out


def ssd_scan(x, dt, a, bm, cm, h0, chunk):
    b, l, nh, hp = x.shape
    g, n = bm.shape[2], bm.shape[3]
    r = nh // g
    nc = l // chunk
    xc = x.reshape(b, nc, chunk, g, r, hp)
    dtc = dt.reshape(b, nc, chunk, g, r)
    bc = bm.reshape(b, nc, chunk, g, n)
    cc = cm.reshape(b, nc, chunk, g, n)
    acs = jnp.cumsum(dtc * a.reshape(g, r), axis=2)
    diff = acs[:, :, :, None] - acs[:, :, None, :]
    causal = jnp.tril(jnp.ones((chunk, chunk), dtype=bool))[:, :, None, None]
    decay = jnp.exp(jnp.where(causal, diff, -jnp.inf))
    cb = jnp.einsum('bcign,bcjgn->bcijg', cc, bc)
    y_diag = jnp.einsum('bcijg,bcijgr,bcjgr,bcjgrp->bcigrp', cb, decay, dtc, xc)
    w_state = jnp.exp(acs[:, :, -1:] - acs) * dtc
    states = jnp.einsum('bcjgn,bcjgr,bcjgrp->bcgrpn', bc, w_state, xc)
    chunk_decay = jnp.exp(acs[:, :, -1])

    def step(h, inp):
        dec, st = inp
        return dec[..., None, None] * h + st, h

    h_last, h_prev = lax.scan(step, h0.reshape(b, g, r, hp, n), (jnp.transpose(chunk_decay, (1, 0, 2, 3)), jnp.transpose(states, (1, 0, 2, 3, 4, 5))))
    h_prev = jnp.transpose(h_prev, (1, 0, 2, 3, 4, 5))
    y_off = jnp.einsum('bcign,bcigr,bcgrpn->bcigrp', cc, jnp.exp(acs), h_prev)
    y = (y_diag + y_off).reshape(b, l, nh, hp)
    return y, h_last.reshape(b, nh, hp, n)


def layer(x, c, cache_k, cache_v, cache_ki, state_conv, state_ssm, w_ada, b_ada, g_norm_mix, g_norm_ffn, w_in, g_q, g_k, w_conv, b_conv, dt_bias, a_log, d_skip, g_ssm_norm, w_branch_attn, w_branch_ssm, w_out, w_gate_up, w_down):
    b, t, _ = x.shape
    pos = cache_k.shape[1] + jnp.arange(t)
    mod = (jax.nn.silu(c) @ w_ada + b_ada)[:, None, :]
    sh1, sc1, gt1, sh2, sc2, gt2 = jnp.split(mod, 6, axis=-1)
    h = rms_normalize(x) * g_norm_mix * (1.0 + sc1) + sh1
    offs = np.cumsum(IN_SIZES)[:-1].tolist()
    q, k, v, qi, ki, wi, z, xbc, dt, gate_logits = jnp.split(h @ w_in, offs, axis=-1)
    q = rope(rms_normalize(q.reshape(b, t, N_HEADS, HEAD_DIM)) * g_q, pos)
    k = rope(rms_normalize(k.reshape(b, t, N_KV_HEADS, HEAD_DIM)) * g_k, pos)
    v = v.reshape(b, t, N_KV_HEADS, HEAD_DIM)
    qi = rope(qi.reshape(b, t, IDX_HEADS, IDX_DIM), pos)
    ki = rope(ki[:, :, None, :], pos)[:, :, 0, :]
    wi = wi * (IDX_HEADS ** -0.5)
    o_attn = dsa_attention(q, qi, wi, pos, jnp.concatenate([cache_k, k], axis=1), jnp.concatenate([cache_v, v], axis=1), jnp.concatenate([cache_ki, ki], axis=1))
    full = jnp.concatenate([state_conv, xbc], axis=1)
    xbc_c = jax.nn.silu(causal_conv(full, w_conv, b_conv, t))
    xs, bm, cm = jnp.split(xbc_c, [D_INNER, D_INNER + SSM_GROUPS * D_STATE], axis=-1)
    xh = xs.reshape(b, t, SSM_HEADS, SSM_HEAD_DIM)
    dtp = jax.nn.softplus((dt + dt_bias).astype(jnp.float32))
    y, h_last = ssd_scan(xh.astype(jnp.float32), dtp, -jnp.exp(a_log.astype(jnp.float32)), bm.reshape(b, t, SSM_GROUPS, D_STATE).astype(jnp.float32), cm.reshape(b, t, SSM_GROUPS, D_STATE).astype(jnp.float32), state_ssm.astype(jnp.float32), min(SSD_CHUNK, t))
    y = (y.astype(x.dtype) + d_skip[:, None] * xh).reshape(b, t, D_INNER) * jax.nn.silu(z)
    y = rms_normalize(y.reshape(b, t, SSM_GROUPS, D_INNER // SSM_GROUPS)).reshape(b, t, D_INNER) * g_ssm_norm
    g_attn, g_ssm = jnp.split(jax.nn.sigmoid(gate_logits), 2, axis=-1)
    mixed = (g_attn * (o_attn @ w_branch_attn) + g_ssm * (y @ w_branch_ssm)) @ w_out
    x = x + gt1 * mixed
    h2 = rms_normalize(x) * g_norm_ffn * (1.0 + sc2) + sh2
    gate, up = jnp.split(h2 @ w_gate_up, 2, axis=-1)
    x = x + gt2 * ((jax.nn.silu(gate) * up) @ w_down)
    return x, k, v, ki, full[:, -(CONV_W - 1):], h_last.astype(x.dtype)


def setup_inputs(seed: int = 0) -> dict:
    key = jax.random.key(seed)
    ks = iter(jax.random.split(key, 40))
    f32 = jnp.float32

    def nrm(shape, scale):
        return jax.random.normal(next(ks), shape, f32) * scale

    def gain(shape):
        return 1.0 + nrm(shape, 0.02)

    dt0 = jnp.exp(jax.random.uniform(next(ks), (DEPTH, SSM_HEADS), f32, math.log(1e-3), math.log(1e-1)))
    dt_bias = dt0 + jnp.log(-jnp.expm1(-dt0))
    a_log = jnp.log(jax.random.uniform(next(ks), (DEPTH, SSM_HEADS), f32, 1.0, 16.0))
    return {
        'x_prompt': nrm((BATCH, SEQ, D_MODEL), 1.0),
        'x_sample': nrm((DEC_BATCH, DEC_SEQ, D_MODEL), 1.0),
        'cache_k': nrm((DEPTH, DEC_BATCH, PAST_LEN, N_KV_HEADS, HEAD_DIM), 1.0),
        'cache_v': nrm((DEPTH, DEC_BATCH, PAST_LEN, N_KV_HEADS, HEAD_DIM), 1.0),
        'cache_ki': nrm((DEPTH, DEC_BATCH, PAST_LEN, IDX_DIM), 1.0),
        'state_conv': nrm((DEPTH, DEC_BATCH, CONV_W - 1, CONV_CH), 1.0),
        'state_ssm': nrm((DEPTH, DEC_BATCH, SSM_HEADS, SSM_HEAD_DIM, D_STATE), 0.1),
        'c_prompt': nrm((BATCH, D_MODEL), 1.0),
        'c_sample': nrm((DEC_BATCH, D_MODEL), 1.0),
        'w_ada': nrm((DEPTH, D_MODEL, 6 * D_MODEL), 0.5 * D_MODEL ** -0.5),
        'b_ada': nrm((DEPTH, 6 * D_MODEL), 0.02),
        'g_norm_mix': gain((DEPTH, D_MODEL)),
        'g_norm_ffn': gain((DEPTH, D_MODEL)),
        'w_in': nrm((DEPTH, D_MODEL, IN_DIM), D_MODEL ** -0.5),
        'g_q': gain((DEPTH, HEAD_DIM)),
        'g_k': gain((DEPTH, HEAD_DIM)),
        'w_conv': nrm((DEPTH, CONV_W, CONV_CH), CONV_W ** -0.5),
        'b_conv': nrm((DEPTH, CONV_CH), 0.02),
        'dt_bias': dt_bias,
        'a_log': a_log,
        'd_skip': gain((DEPTH, SSM_HEADS)),
        'g_ssm_norm': gain((DEPTH, D_INNER)),
        'w_branch_attn': nrm((DEPTH, N_HEADS * HEAD_DIM, D_MODEL), (N_HEADS * HEAD_DIM) ** -0.5),
        'w_branch_ssm': nrm((DEPTH, D_INNER, D_MODEL), D_INNER ** -0.5),
        'w_out': nrm((DEPTH, D_MODEL, D_MODEL), D_MODEL ** -0.5),
        'w_gate_up': nrm((DEPTH, D_MODEL, 2 * D_FF), D_MODEL ** -0.5),
        'w_down': nrm((DEPTH, D_FF, D_MODEL), D_FF ** -0.5),
    }


def reference(x_prompt, x_sample, cache_k, cache_v, cache_ki, state_conv, state_ssm, c_prompt, c_sample, w_ada, b_ada, g_norm_mix, g_norm_ffn, w_in, g_q, g_k, w_conv, b_conv, dt_bias, a_log, d_skip, g_ssm_norm, w_branch_attn, w_branch_ssm, w_out, w_gate_up, w_down):
    bp = x_prompt.shape[0]
    dtype = x_prompt.dtype
    y_p, y_s = x_prompt, x_sample
    new_p = [[], [], [], [], []]
    new_s = [[], [], [], [], []]
    for l in range(DEPTH):
        lw = (w_ada[l], b_ada[l], g_norm_mix[l], g_norm_ffn[l], w_in[l], g_q[l], g_k[l], w_conv[l], b_conv[l], dt_bias[l], a_log[l], d_skip[l], g_ssm_norm[l], w_branch_attn[l], w_branch_ssm[l], w_out[l], w_gate_up[l], w_down[l])
        y_p, *st_p = layer(y_p, c_prompt, jnp.zeros((bp, 0, N_KV_HEADS, HEAD_DIM), dtype), jnp.zeros((bp, 0, N_KV_HEADS, HEAD_DIM), dtype), jnp.zeros((bp, 0, IDX_DIM), dtype), jnp.zeros((bp, CONV_W - 1, CONV_CH), dtype), jnp.zeros((bp, SSM_HEADS, SSM_HEAD_DIM, D_STATE), dtype), *lw)
        y_s, *st_s = layer(y_s, c_sample, cache_k[l], cache_v[l], cache_ki[l], state_conv[l], state_ssm[l], *lw)
        for acc, a in zip(new_p, st_p):
            acc.append(a)
        for acc, a in zip(new_s, st_s):
            acc.append(a)
    return (y_p, y_s, jnp.stack(new_p[0]), jnp.stack(new_p[1]), jnp.stack(new_p[2]), jnp.stack(new_p[3]), jnp.stack(new_p[4]), jnp.stack(new_s[0]), jnp.stack(new_s[1]), jnp.stack(new_s[2]), jnp.stack(new_s[3]), jnp.stack(new_s[4]))
```

```python
import contextlib
import math
import numpy as np
import concourse.bass as bass
import concourse.mybir as mybir
from concourse.bass_utils import run_bass_kernel_spmd

F32 = mybir.dt.float32
BF16 = mybir.dt.bfloat16
ALU = mybir.AluOpType
AF = mybir.ActivationFunctionType
AX = mybir.AxisListType

ENGS = ("pe", "act", "dve", "pool", "sp")
EPS = 1e-6
WBUF_ELEMS = 6144


class Cfg:
    def __init__(self, D=1024, NH=16, NKV=4, IH=8, DI=2048, SG=8, FF=2816, SEQ=2048, T=512, NS=4, DS=64,
                 PAST=2048, NIT=16, TOPK_MAX=256, THETA=10000.0):
        self.D, self.NH, self.NKV, self.IH, self.DI, self.SG, self.FF = D, NH, NKV, IH, DI, SG, FF
        self.SEQ, self.T, self.NS, self.DS, self.PAST, self.NIT = SEQ, T, NS, DS, PAST, NIT
        self.THETA = THETA
        self.DC = D // 128
        self.QC = NH // 2
        self.IC = IH // 2
        self.XC = DI // 128
        self.SH = DI // 64
        self.FC = FF // 128
        self.NSEQ = 1 + NS
        self.TT = SEQ + NS * DS
        self.NT = SEQ // T
        self.KP = min(TOPK_MAX, SEQ // 4)
        self.KS = min(TOPK_MAX, (PAST + DS) // 4)
        self.NTOK = NKV * 64 + IH + self.SH
        self.NXB = self.XC + 2 * SG
        self.LMAX = max(SEQ, PAST + DS)
        self.NBMAX = (self.LMAX + 127) // 128
        assert NH == 4 * NKV and self.SH == 4 * SG and IH % 2 == 0 and SEQ % T == 0 and T % 128 == 0
        assert PAST % 128 == 0 and DS == 64 and FF % 128 == 0
        o = {}
        n = 0
        for name, w in (("gmix", self.DC), ("gffn", self.DC), ("bada", 6 * self.DC), ("gq", 1), ("gk", 1),
                        ("convw", self.NXB * 4), ("convb", self.NXB), ("dskip", self.XC), ("gssm", self.XC),
                        ("dtb", self.SH), ("alog", self.SH)):
            o[name] = (n, w)
            n += w
        self.vec_off, self.NV = o, n
        o = {}
        n = 0
        for name, w in (("tri", 128), ("ones", 128), ("ident", 128), ("mb", 128), ("mb2", 128), ("pow2", 32)):
            o[name] = (n, w)
            n += w
        self.cf_off, self.NF = o, n
        o = {}
        n = 0
        for name, w in (("ident", 128), ("ones", 128), ("blk", 128), ("rot", 128)):
            o[name] = (n, w)
            n += w
        self.cb_off, self.NB = o, n

    def streams(self):
        c = self
        def grp(ids, k):
            return [ids[i:i + k] for i in range(0, len(ids), k)]
        s = {}
        s["ada"] = (c.DC, grp(list(range(6 * c.DC)), 6))
        s["b1"] = (c.DC, grp(list(range(c.QC + c.NKV + c.IC + 1)), 6))
        s["b2"] = (c.DC, [list(range(6 * g, 6 * g + 6)) for g in range(c.SG)])
        s["gate"] = (c.DC, grp(list(range(2 * c.DC)), 6))
        s["ba"] = (c.QC, grp(list(range(c.DC)), max(1, min(6, WBUF_ELEMS // (c.QC * 128)))))
        s["bs"] = (c.XC, grp(list(range(c.DC)), max(1, min(6, WBUF_ELEMS // (c.XC * 128)))))
        s["out"] = (c.DC, grp(list(range(c.DC)), 6))
        npair = max(1, min(3, WBUF_ELEMS // (c.DC * 256)))
        s["gu"] = (c.DC, [sum([[2 * j for j in p], [2 * j + 1 for j in p]], []) for p in grp(list(range(c.FC)), npair)])
        s["dn"] = (c.FC, grp(list(range(c.DC)), max(1, min(6, WBUF_ELEMS // (c.FC * 128)))))
        return s


FULL = Cfg()


class Prog:
    LAT = 0.25
    DMA_FIX = 2.0
    DMA_BW = 150e3

    def __init__(self, nc, n_dma_sems=32, schedule=True):
        self.nc = nc
        self.schedule = schedule
        self.seg = []
        self.count = {e: 0 for e in ENGS}
        self.known = {e: {} for e in ENGS}
        self.res = {}
        self.n_dma_sems = n_dma_sems
        self.dma_val = [0] * n_dma_sems
        self.dma_last = [None] * n_dma_sems
        self.dma_rr = 0
        self.dma_rr_pool = 0
        self.out_sigs = []
        self.ninstr = 0
        self.nid = 0
        self.pre = {e: [] for e in ENGS}

    @staticmethod
    def _overlap(a, b):
        n = min(len(a), len(b))
        return a[:n] == b[:n]

    @staticmethod
    def _norm(keys):
        out = []
        for k in keys or []:
            if isinstance(k, str):
                k = (k,)
            out.append(tuple(k))
        return out

    def _deps_for(self, eng, reads, writes):
        deps = []
        for key in reads:
            ent = self.res.get(key[0], {})
            for k2, (w, rs) in ent.items():
                if self._overlap(key, k2):
                    if w is not None:
                        deps.append(w)
                    if key[0] == "ps":
                        deps.extend(r for r in rs if r["eng"] != eng)
        for key in writes:
            ent = self.res.get(key[0], {})
            for k2, (w, rs) in ent.items():
                if self._overlap(key, k2):
                    if w is not None:
                        deps.append(w)
                    deps.extend(rs)
        return deps

    def _record(self, op, reads, writes):
        for key in reads:
            ent = self.res.setdefault(key[0], {})
            if key not in ent:
                ent[key] = [None, []]
            ent[key][1].append(op)
        for key in writes:
            ent = self.res.setdefault(key[0], {})
            for k2 in [k for k in ent if self._overlap(key, k) and k != key and len(k) > len(key)]:
                del ent[k2]
            ent[key] = [op, []]

    def _add(self, eng, fn, kind, reads, writes, dur, is_output=False):
        reads = self._norm(reads)
        writes = self._norm(writes)
        deps = self._deps_for(eng, reads, writes)
        op = {"id": self.nid, "eng": eng, "fn": fn, "kind": kind, "deps": {d["id"]: d for d in deps}, "dur": dur, "out": is_output}
        self.nid += 1
        self.seg.append(op)
        self._record(op, reads, writes)
        self.ninstr += 1
        return op

    def op(self, eng, fn, reads=None, writes=None, dur=0.3, tag=None):
        o = self._add(eng, fn, "eng", reads, writes, dur)
        o["tag"] = tag
        return o

    def dma(self, eng, fn, reads=None, writes=None, is_output=False, nbytes=1 << 20):
        return self._add(eng, fn, "dma", reads, writes, nbytes, is_output)

    def open(self, stack):
        nc = self.nc
        self.sems = {}
        for e in ENGS:
            self.sems[e] = stack.enter_context(nc.semaphore("p_" + e))
        for i in range(self.n_dma_sems):
            self.sems[("dma", i)] = stack.enter_context(nc.semaphore("d_%d" % i))

    def _order(self, ops):
        if not self.schedule:
            return {e: [o for o in ops if o["eng"] == e] for e in ENGS}
        import heapq
        ids = {o["id"] for o in ops}
        succ = {o["id"]: [] for o in ops}
        indeg = {}
        for o in ops:
            d = [k for k in o["deps"] if k in ids]
            indeg[o["id"]] = len(d)
            for k in d:
                succ[k].append(o)
        fin = {}
        cur_tab = [None]
        free = {e: 0.0 for e in ENGS}
        order = {e: [] for e in ENGS}
        ready_t = {o["id"]: 0.0 for o in ops}
        heap = [(0.0, o["id"], o) for o in ops if indeg[o["id"]] == 0]
        heapq.heapify(heap)
        while heap:
            rt, oid, o = heapq.heappop(heap)
            e = o["eng"]
            est = max(rt, free[e])
            tg = o.get("tag")
            if tg is not None and tg != cur_tab[0] and not o.get("_tq"):
                o["_tq"] = True
                heapq.heappush(heap, (est + 1.3, oid, o))
                continue
            if est > rt + 1e-9:
                if not o.get("_rq"):
                    o["_rq"] = True
                    heapq.heappush(heap, (est, oid, o))
                    continue
            start = est
            if tg is not None and tg != cur_tab[0]:
                cur_tab[0] = tg
                start += 1.3
            if o["kind"] == "dma":
                free[e] = start + 0.1
                done = start + self.DMA_FIX + o["dur"] / self.DMA_BW
            else:
                free[e] = start + o["dur"]
                done = free[e]
            fin[oid] = done
            order[e].append(o)
            for s in succ[oid]:
                lat = 0.1 if s["eng"] == e and o["kind"] != "dma" else self.LAT
                ready_t[s["id"]] = max(ready_t[s["id"]], done + lat)
                indeg[s["id"]] -= 1
                if indeg[s["id"]] == 0:
                    heapq.heappush(heap, (ready_t[s["id"]], s["id"], s))
        assert sum(len(v) for v in order.values()) == len(ops)
        return order

    def _waits(self, eng, sigs):
        best = {}
        for (sk, v) in sigs:
            if sk == "pe" and eng == "pe":
                continue
            if v > best.get(sk, 0):
                best[sk] = v
        waits = []
        for sk, v in best.items():
            if self.known[eng].get(sk, 0) >= v:
                continue
            self.known[eng][sk] = v
            waits.append((sk, v))
        return waits

    def barrier(self):
        self._emit_segment()
        toks = [(e, self.count[e]) for e in ENGS if e != "sp" and self.count[e] > 0]
        toks += [(("dma", i), v) for i, v in enumerate(self.dma_val) if v > 0]
        for e in ENGS:
            w = self._waits(e, [t for t in toks if t[0] != e])
            if w:
                self.pre[e].extend(w)
        self.res = {}

    def flush(self, final=False):
        if final:
            self._emit_segment(final=True)

    def _emit_segment(self, final=False):
        ops = self.seg
        self.seg = []
        order = self._order(ops)
        half = self.n_dma_sems // 2
        for e in ENGS:
            for o in order[e]:
                if o["kind"] == "eng":
                    self.count[e] += 1
                    o["sig"] = (e, self.count[e])
                else:
                    if e == "pool":
                        i = half + self.dma_rr_pool
                        self.dma_rr_pool = (self.dma_rr_pool + 1) % (self.n_dma_sems - half)
                    else:
                        i = self.dma_rr
                        self.dma_rr = (self.dma_rr + 1) % half
                    o["reuse"] = self.dma_last[i]
                    self.dma_val[i] += 16
                    o["sig"] = (("dma", i), self.dma_val[i])
                    self.dma_last[i] = o["sig"]
                    if o["out"]:
                        self.out_sigs.append(o["sig"])
        seg_ids = {o["id"] for o in ops}
        prog = {}
        for e in ENGS:
            lst = []
            pre = self.pre[e]
            self.pre[e] = []
            for o in order[e]:
                sigs = [d["sig"] for k, d in o["deps"].items() if k in seg_ids]
                if o["kind"] == "dma" and o["reuse"] is not None:
                    sigs.append(o["reuse"])
                lst.append((self._waits(e, sigs), o))
            prog[e] = (pre, lst)
        fin = self._waits("sp", self.out_sigs) if final else None
        nc = self.nc
        sems = self.sems

        def run(ename, h, fin_=None):
            pre, lst = prog[ename]
            for sk, v in pre:
                h.wait_ge(sems[sk], v)
            for waits, o in lst:
                for sk, v in waits:
                    h.wait_ge(sems[sk], v)
                ins = o["fn"](h)
                ins.then_inc(sems[o["sig"][0]], 16 if o["kind"] == "dma" else 1)
            if fin_:
                for sk, v in fin_:
                    h.wait_ge(sems[sk], v)

        def has(e):
            return bool(prog[e][0] or prog[e][1])

        if not any(has(e) for e in ENGS) and not fin:
            return
        with nc.Block() as block:
            if has("pe"):
                @block.tensor
                def _(h):
                    run("pe", h)
            if has("act"):
                @block.scalar
                def _(h):
                    run("act", h)
            if has("dve"):
                @block.vector
                def _(h):
                    run("dve", h)
            if has("pool"):
                @block.gpsimd
                def _(h):
                    run("pool", h)
            if has("sp") or fin:
                @block.sync
                def _(h):
                    run("sp", h, fin)


def _ap(ap, pat):
    return bass.AP(ap.tensor, ap.offset, pat)


def bc_mid(ap, n):
    a = [list(x) for x in ap.ap]
    return _ap(ap, [a[0], [0, n]] + a[1:])


def bc_last(ap, n):
    a = [list(x) for x in ap.ap]
    return _ap(ap, a + [[0, n]])


class _Stop(Exception):
    pass


def build(cfg, debug=(), limit=None):
    c = cfg
    nc = bass.Bass("TRN2", target_bir_lowering=False)
    DC, QC, IC, XC, SH, SG, FC, NKV, IH = c.DC, c.QC, c.IC, c.XC, c.SH, c.SG, c.FC, c.NKV, c.IH
    NSEQ, TT, T, NS, DS, PAST, SEQ = c.NSEQ, c.TT, c.T, c.NS, c.DS, c.PAST, c.SEQ
    NXB = c.NXB
    streams = c.streams()

    def din(name, shape):
        return nc.dram_tensor(name, list(shape), F32, kind="ExternalInput").ap()

    def dout(name, shape):
        return nc.dram_tensor(name, list(shape), F32, kind="ExternalOutput").ap()

    xT_d = din("xT", [128, DC, TT])
    cT_d = din("cT", [128, DC, NSEQ])
    rope_d = din("rope", [128, 2, TT])
    cf_d = din("cf32", [128, c.NF])
    cb_d = din("cbf", [128, c.NB])
    vec_d = din("vecs", [128, c.NV])
    w_d = {}
    for sname, (KC, pieces) in streams.items():
        mx = max(len(p) for p in pieces) * 128 * KC
        w_d[sname] = din("w_" + sname, [len(pieces), 128, mx])
    wtok_d = din("w_tok", [128, DC * c.NTOK])
    ckT_d = din("ckT", [NS, 64, NKV, PAST])
    ckiT_d = din("ckiT", [NS, 64, PAST])
    cv_d = din("cv", [NS, 128, PAST // 128, NKV * 64])
    sconv_d = din("sconv", [128, NXB, NS, 3])
    sssm_d = din("sssm", [NS, SG, 128, 256])
    o_y = dout("o_y", [128, DC, TT])
    o_k = dout("o_k", [64, NKV, TT])
    o_ki = dout("o_ki", [64, TT])
    o_v = dout("o_v", [TT, NKV * 64])
    o_conv = dout("o_conv", [128, NXB, NSEQ, 3])
    o_ssm = dout("o_ssm", [NSEQ, SG, 128, 256])
    dbg_outs = {}

    with contextlib.ExitStack() as st:
        P = Prog(nc)
        P.open(st)

        uid = {"n": 0}

        def sb(name, shape, dt, stack=st):
            uid["n"] += 1
            return stack.enter_context(nc.sbuf_tensor("%s_%d" % (name, uid["n"]), list(shape), dt))

        def DM(eng, out, in_, reads=None, writes=None, is_output=False):
            return P.dma(eng, lambda h: h.dma_start(out=out, in_=in_), reads=reads, writes=writes, is_output=is_output)

        ps = [st.enter_context(nc.psum_tensor("ps%d" % i, [128, 512], F32)) for i in range(8)]
        psb = ps[7][:, :].bitcast(BF16)

        cf = sb("cf", [128, c.NF], F32)
        cb = sb("cb", [128, c.NB], BF16)
        vec = sb("vec", [128, c.NV], F32)
        NWB = 4
        wbuf = [sb("wbuf%d" % i, [128, WBUF_ELEMS], BF16) for i in range(NWB)]
        wtok = sb("wtok", [128, DC, c.NTOK], BF16)
        modT = sb("modT", [128, 6 * DC, NSEQ], F32)
        G1 = sb("G1", [128, DC, NSEQ], F32)
        G2 = sb("G2", [128, DC, NSEQ], F32)
        a_b = sb("a_b", [128, SH], F32)
        xT = sb("xT", [128, DC, T], F32)
        hT = sb("hT", [128, DC, T], BF16)
        qT = sb("qT", [128, QC, T], BF16)
        y3T = sb("y3T", [128, XC, T], BF16)
        kT = sb("kT", [128, NKV, c.LMAX], BF16)
        kiT = sb("kiT", [128, c.LMAX], BF16)
        Vc = sb("Vc", [128, c.NBMAX, NKV * 64], BF16)
        hst = sb("hst", [128, SG, 256], F32)
        chist = sb("chist", [128, NXB, 3], BF16)
        cst = sb("cst", [128, NXB, NSEQ, 3], F32)
        NBLK = T // 128 if T // 128 > NS else NS
        wis = sb("wis", [128, NBLK, IH], F32)
        dtp = sb("dtp", [128, NBLK, SH], F32)
        dAt = sb("dAt", [128, NBLK, SH], F32)

        def cfv(name, a=0, b=None):
            o, w = c.cf_off[name]
            return cf[:, o + a: o + (w if b is None else b)]

        def cbv(name):
            o, w = c.cb_off[name]
            return cb[:, o:o + w]

        def vv(name, a=0, b=None):
            o, w = c.vec_off[name]
            return vec[:, o + a: o + (w if b is None else b)]

        def dbg(name, ap, shape, key):
            if name not in debug:
                return
            d = nc.dram_tensor("dbg_" + name, list(shape), ap.dtype, kind="ExternalOutput").ap()
            dbg_outs[name] = d
            P.dma("sp", lambda h: h.dma_start(out=d, in_=ap), reads=[key], is_output=True)

        plan = []
        plan += [("ada", i) for i in range(len(streams["ada"][1]))]

        def tile_plan():
            pl = []
            for s in ("b1", "b2", "gate", "ba", "bs", "out", "gu", "dn"):
                pl += [(s, i) for i in range(len(streams[s][1]))]
            return pl
        import os as _os
        _nt = len(_os.environ["TILES"].split(",")) if "TILES" in _os.environ else c.NT + 1
        for _ in range(_nt):
            plan += tile_plan()
        wq = {"next_issue": 0, "next_use": 0}

        def w_issue(k):
            sname, pi = plan[k]
            KC, pieces = streams[sname]
            n = KC * len(pieces[pi]) * 128
            slot = k % NWB
            src = w_d[sname][pi, :, 0:n]
            dst = wbuf[slot][:, 0:n]
            P.dma("pool", lambda h: h.dma_start(out=dst, in_=src), writes=[("wbuf", slot)])

        def w_get(sname, pi):
            k = wq["next_use"]
            assert plan[k] == (sname, pi), (plan[k], sname, pi)
            while wq["next_issue"] < min(len(plan), k + NWB):
                w_issue(wq["next_issue"])
                wq["next_issue"] += 1
            wq["next_use"] += 1
            KC, pieces = streams[sname]
            ncols = len(pieces[pi]) * 128
            slot = k % NWB
            view = wbuf[slot][:, 0:KC * ncols].rearrange("p (k n) -> p k n", k=KC)
            return view, ("wbuf", slot)

        dense_rr = {"i": 0}
        phase = {"n": 0}

        def phase_done():
            phase["n"] += 1
            return limit is not None and phase["n"] >= limit

        def dense_fm(sname, KC, rhs_fn, TW, handler, rd_keys):
            _, pieces = streams[sname]
            for pi, chunks in enumerate(pieces):
                wv, wkey = w_get(sname, pi)
                for j, cid in enumerate(chunks):
                    bank = dense_rr["i"] % 3
                    dense_rr["i"] += 1
                    for kc in range(KC):
                        lhsT = wv[:, kc, j * 128:(j + 1) * 128]
                        rhs = rhs_fn(kc)
                        out = ps[bank][:, 0:TW]
                        mm(out, lhsT, rhs, kc == 0, kc == KC - 1, [wkey] + rd_keys, [("ps", bank)])
                    handler(cid, ps[bank][:, 0:TW], ("ps", bank))

        def est(eng, meth, a, kw):
            try:
                if eng == "pe":
                    rhs = a[2] if meth == "matmul" else a[1]
                    n = rhs.free_size()
                    if meth == "matmul":
                        return 0.03 + max(64, n) * (4 if rhs.dtype == F32 else 1) / 2400.0
                    return 0.12
                ap = kw.get("out", a[0] if a else None)
                n = ap.free_size()
                if eng == "act":
                    return 0.22 + n / 1200.0
                if eng == "dve":
                    return 0.10 + n / 960.0
                return 0.20 + 2.0 * n / 1200.0
            except Exception:
                return 0.3

        TABS = {AF.Silu: "silu", AF.Sigmoid: "sigmoid", AF.Exp: "exp", AF.Ln: "exp"}

        def E(eng, meth, reads, writes, *a, **kw):
            tag = TABS.get(kw.get("func")) if eng == "act" else None
            return P.op(eng, lambda h: getattr(h, meth)(*a, **kw), reads=reads, writes=writes, dur=est(eng, meth, a, kw), tag=tag)

        def act(out, in_, func, reads, writes, **kw):
            return E("act", "activation", reads, writes, out=out, in_=in_, func=func, **kw)

        def tt(eng, out, in0, in1, op, reads, writes):
            return E(eng, "tensor_tensor", reads, writes, out=out, in0=in0, in1=in1, op=op)

        def ts(eng, out, in0, s1, s2, op0, op1, reads, writes, **kw):
            if op1 is None:
                return E(eng, "tensor_scalar", reads, writes, out=out, in0=in0, scalar1=s1, scalar2=None, op0=op0, **kw)
            return E(eng, "tensor_scalar", reads, writes, out=out, in0=in0, scalar1=s1, scalar2=s2, op0=op0, op1=op1, **kw)

        def stt(eng, out, in0, scalar, in1, op0, op1, reads, writes):
            return E(eng, "scalar_tensor_tensor", reads, writes, out=out, in0=in0, scalar=scalar, in1=in1, op0=op0, op1=op1)

        def mm(out, lhsT, rhs, start, stop, reads, writes, **kw):
            return E("pe", "matmul", reads, writes, out, lhsT, rhs, start=start, stop=stop, **kw)

        def mmg(items, reads, writes):
            def fn(h):
                ins = None
                for (o, l, r, s0, s1, kw) in items:
                    ins = h.matmul(o, l, r, start=s0, stop=s1, **kw)
                return ins
            d = sum(est("pe", "matmul", (o, l, r), {}) for (o, l, r, s0, s1, kw) in items) * 0.6
            return P.op("pe", fn, reads=reads, writes=writes, dur=d)

        P.dma("sp", lambda h: h.dma_start(out=cf[:], in_=cf_d), writes=["cf"])
        P.dma("pool", lambda h: h.dma_start(out=cb[:], in_=cb_d), writes=["cb"])
        P.dma("sp", lambda h: h.dma_start(out=vec[:], in_=vec_d), writes=["vec"])
        P.dma("pool", lambda h: h.dma_start(out=wtok[:].rearrange("p k n -> p (k n)"), in_=wtok_d), writes=["wtok"])
        epsc = sb("epsc", [128, 1], F32)
        E("dve", "memset", [], ["epsc"], epsc[:], EPS)
        E("dve", "memset", [], ["hst"], hst[:], 0.0)
        E("dve", "memset", [], ["chist"], chist[:], 0.0)
        E("dve", "memset", [], ["cst"], cst[:], 0.0)
        with contextlib.ExitStack() as s0:
            cTs = sb("cTs", [128, DC, NSEQ], F32, s0)
            scT = sb("scT", [128, DC, NSEQ], BF16, s0)
            P.dma("sp", lambda h: h.dma_start(out=cTs[:], in_=cT_d), writes=["cTs"])
            act(scT[:], cTs[:], AF.Silu, ["cTs"], ["scT"])
            act(a_b[:], vv("alog"), AF.Exp, ["vec"], ["a_b"])
            ts("dve", a_b[:], a_b[:], -1.0, None, ALU.mult, None, ["a_b"], ["a_b"])
            _, pieces = streams["ada"]
            for pi, chunks in enumerate(pieces):
                wv, wkey = w_get("ada", pi)
                for j, cid in enumerate(chunks):
                    for kc in range(DC):
                        mm(ps[3][:, cid * NSEQ:(cid + 1) * NSEQ], wv[:, kc, j * 128:(j + 1) * 128], scT[:, kc, :],
                           kc == 0, kc == DC - 1, [wkey, "scT"], [("ps", 3)], skip_group_check=True)
            tt("dve", modT[:], ps[3][:, 0:6 * DC * NSEQ].rearrange("p (a b) -> p a b", b=NSEQ),
               bc_last(vv("bada"), NSEQ), ALU.add, [("ps", 3), "vec"], ["modT"])
            stt("dve", G1[:], modT[:, 1 * DC:2 * DC, :], 1.0, bc_last(vv("gmix"), NSEQ), ALU.add, ALU.mult, ["modT", "vec"], ["G1"])
            stt("dve", G2[:], modT[:, 4 * DC:5 * DC, :], 1.0, bc_last(vv("gffn"), NSEQ), ALU.add, ALU.mult, ["modT", "vec"], ["G2"])
            dbg("modT", modT[:], [128, 6 * DC, NSEQ], "modT")
            P.barrier()
            P.flush()
        phase_done_flag = True

        SH1 = lambda ch, s: modT[:, 0 * DC + ch, s:s + 1]
        GT1 = lambda ch, s: modT[:, 2 * DC + ch, s:s + 1]
        SH2 = lambda ch, s: modT[:, 3 * DC + ch, s:s + 1]
        GT2 = lambda ch, s: modT[:, 5 * DC + ch, s:s + 1]

        def rms_mod(src, G, SHf, dst, TW, segs, sc):
            sqb = [sb("rm_sq%d" % i, [128, T], BF16, sc) for i in range(2)]
            rs1 = sb("rm_rs1", [128, T], F32, sc)
            rstd = sb("rm_rstd", [128, T], F32, sc)
            tmp = [sb("rm_tmp%d" % i, [128, T], F32, sc) for i in range(2)]
            for ch in range(DC):
                act(sqb[ch % 2][:, 0:TW], src[:, ch, 0:TW], AF.Square, ["xT"], [("rm_sq", ch % 2)])
                mm(ps[3][:, 0:TW], cbv("ones"), sqb[ch % 2][:, 0:TW], ch == 0, ch == DC - 1, [("rm_sq", ch % 2), "cb"], [("ps", 3)])
            act(rs1[:, 0:TW], ps[3][:, 0:TW], AF.Ln, [("ps", 3)], ["rm_rs1"], scale=1.0 / c.D, bias=epsc[:, 0:1])
            act(rstd[:, 0:TW], rs1[:, 0:TW], AF.Exp, ["rm_rs1"], ["rm_rstd"], scale=-0.5)
            for ch in range(DC):
                tt("dve", tmp[ch % 2][:, 0:TW], src[:, ch, 0:TW], rstd[:, 0:TW], ALU.mult, ["xT", "rm_rstd"], [("rm_tmp", ch % 2)])
                for (s, t0, ln) in segs:
                    act(dst[:, ch, t0:t0 + ln], tmp[ch % 2][:, t0:t0 + ln], AF.Identity, [("rm_tmp", ch % 2), "modT", "G1", "G2"],
                        [("hT", ch)], scale=G[:, ch, s:s + 1], bias=SHf(ch, s))

        def do_tile(ti):
            is_p = ti < c.NT
            if is_p:
                TW = T
                tok0 = ti * T
                segs = [(0, 0, T)]
                SL = T
            else:
                TW = NS * DS
                tok0 = SEQ
                segs = [(1 + s, s * DS, DS) for s in range(NS)]
                SL = DS
            NSEG = len(segs)
            BL = 128 if is_p else DS
            NB = TW // BL

            with contextlib.ExitStack() as s12:
                knew = sb("knew", [128, NKV, T], BF16, s12)
                qiT = sb("qiT", [128, IC, T], BF16, s12)
                kinew = sb("kinew", [128, T], BF16, s12)
                with contextlib.ExitStack() as s1:
                    ropet = sb("ropet", [128, 2, T], F32, s1)
                    P.dma("sp", lambda h: h.dma_start(out=xT[:, :, 0:TW], in_=xT_d[:, :, tok0:tok0 + TW]), writes=["xT"])
                    P.dma("sp", lambda h: h.dma_start(out=ropet[:, :, 0:TW], in_=rope_d[:, :, tok0:tok0 + TW]), writes=["ropet"])
                    rms_mod(xT, G1, SH1, hT, TW, segs, s1)
                    dbg("hT%d" % ti, hT[:, :, 0:TW], [128, DC, TW], "hT")
                    RD = 3
                    sqb = [sb("b1_sq%d" % i, [128, T], BF16, s1) for i in range(RD)]
                    tA = [sb("b1_tA%d" % i, [128, T], F32, s1) for i in range(RD)]
                    tB = tA
                    tC = [sb("b1_tC%d" % i, [128, T], F32, s1) for i in range(RD)]
                    tD = [sb("b1_tD%d" % i, [128, T], F32, s1) for i in range(RD)]
                    qn = [sb("b1_qn%d" % i, [128, T], BF16, s1) for i in range(RD)]
                    kst = [sb("b1_kst%d" % i, [128, T], F32, s1) for i in range(RD)]
                    vst = [sb("b1_vst%d" % i, [128, c.NTOK], F32, s1) for i in range(2)]
                    sp1 = sb("b1_sp1", [128, SH], F32, s1)
                    sp2 = sb("b1_sp2", [128, SH], F32, s1)
                    rr = {"i": 0}

                    def rope(qn_ap, dst_ap, i, dst_key, extra_reads):
                        rb = 3 + i % 3
                        mm(ps[rb][:, 0:TW], cbv("rot"), qn_ap, True, True, extra_reads + ["cb"], [("ps", rb)])
                        tt("pool", tC[i % RD][:, 0:TW], qn_ap, ropet[:, 0, 0:TW], ALU.mult, extra_reads + ["ropet"], [("b1_tC", i % RD)])
                        tt("dve", tD[i % RD][:, 0:TW], ps[rb][:, 0:TW], ropet[:, 1, 0:TW], ALU.mult, [("ps", rb), "ropet"], [("b1_tD", i % RD)])
                        tt("dve", dst_ap, tC[i % RD][:, 0:TW], tD[i % RD][:, 0:TW], ALU.add, [("b1_tC", i % RD), ("b1_tD", i % RD)], [dst_key])

                    def b1_handler(cid, pp, pkey):
                        i = rr["i"]
                        rr["i"] += 1
                        b = i % RD
                        sqk = 6 + i % 2
                        if cid < QC + NKV:
                            gcol = vv("gq") if cid < QC else vv("gk")
                            act(sqb[b][:, 0:TW], pp, AF.Square, [pkey], [("b1_sq", b)])
                            mm(ps[sqk][:, 0:TW], cbv("blk"), sqb[b][:, 0:TW], True, True, [("b1_sq", b), "cb"], [("ps", sqk)])
                            act(tA[b][:, 0:TW], ps[sqk][:, 0:TW], AF.Ln, [("ps", sqk)], [("b1_tA", b)], scale=1.0 / 64, bias=epsc[:, 0:1])
                            act(tB[b][:, 0:TW], tA[b][:, 0:TW], AF.Exp, [("b1_tA", b)], [("b1_tA", b)], scale=-0.5)
                            stt("dve", qn[b][:, 0:TW], pp, gcol, tB[b][:, 0:TW], ALU.mult, ALU.mult, [pkey, "vec", ("b1_tA", b)], [("b1_qn", b)])
                        else:
                            act(qn[b][:, 0:TW], pp, AF.Copy, [pkey], [("b1_qn", b)])
                        if cid < QC:
                            rope(qn[b][:, 0:TW], qT[:, cid, 0:TW], i, ("qT", cid), [("b1_qn", b)])
                        elif cid < QC + NKV:
                            g = cid - QC
                            rope(qn[b][:, 0:TW], kst[b][:, 0:TW], i, ("b1_kst", b), [("b1_qn", b)])
                            act(knew[:, g, 0:TW], kst[b][:, 0:TW], AF.Copy, [("b1_kst", b)], [("knew", g)])
                            DM("sp", o_k[:, g, tok0:tok0 + TW], kst[b][0:64, 0:TW], reads=[("b1_kst", b)], is_output=True)
                        elif cid < QC + NKV + IC:
                            rope(qn[b][:, 0:TW], qiT[:, cid - QC - NKV, 0:TW], i, ("qiT", cid - QC - NKV), [("b1_qn", b)])
                        else:
                            rope(qn[b][:, 0:TW], kst[b][:, 0:TW], i, ("b1_kst", b), [("b1_qn", b)])
                            act(kinew[:, 0:TW], kst[b][:, 0:TW], AF.Copy, [("b1_kst", b)], ["kinew"])
                            DM("sp", o_ki[:, tok0:tok0 + TW], kst[b][0:64, 0:TW], reads=[("b1_kst", b)], is_output=True)

                    import os as _os
                    SUB = int(_os.environ.get("SUB", "9")) if not is_p else 9
                    if SUB >= 2:
                        dense_fm("b1", DC, lambda kc: hT[:, kc, 0:TW], TW, b1_handler, ["hT"])
                    for blk in range(NB):
                        t0 = blk * BL
                        bank = 4 + blk % 2
                        for kc in range(DC):
                            mm(ps[bank][0:BL, 0:c.NTOK], hT[:, kc, t0:t0 + BL], wtok[:, kc, :], kc == 0, kc == DC - 1,
                               ["hT", "wtok"], [("ps", bank)])
                        b = blk % 2
                        act(vst[b][0:BL, :], ps[bank][0:BL, 0:c.NTOK], AF.Copy, [("ps", bank)], [("b1_vst", b)])
                        pv = vst[b]
                        DM("sp", o_v[tok0 + t0:tok0 + t0 + BL, :], vst[b][0:BL, 0:NKV * 64], reads=[("b1_vst", b)], is_output=True)
                        kb = (tok0 + t0) // 128 if is_p else PAST // 128
                        if is_p:
                            E("pool", "tensor_copy", [("b1_vst", b)], [("Vc", kb)], out=Vc[0:BL, kb, :], in_=pv[0:BL, 0:NKV * 64])
                        else:
                            E("pool", "tensor_copy", [("b1_vst", b)], [("vnew", blk)], out=vnew[0:BL, blk, :], in_=pv[0:BL, 0:NKV * 64])
                        ts("dve", wis[0:BL, blk, :], pv[0:BL, NKV * 64:NKV * 64 + IH], float(IH ** -0.5 * 64 ** -0.5), None, ALU.mult, None,
                           [("b1_vst", b)], [("wis", blk)])
                        tt("dve", sp1[0:BL, :], pv[0:BL, NKV * 64 + IH:c.NTOK], vv("dtb")[0:BL, :], ALU.add, [("b1_vst", b), "vec"], ["b1_sp1"])
                        act(sp2[0:BL, :], sp1[0:BL, :], AF.Abs, ["b1_sp1"], ["b1_sp2"])
                        act(sp2[0:BL, :], sp2[0:BL, :], AF.Exp, ["b1_sp2"], ["b1_sp2"], scale=-1.0)
                        act(sp2[0:BL, :], sp2[0:BL, :], AF.Ln, ["b1_sp2"], ["b1_sp2"], bias=1.0)
                        stt("dve", dtp[0:BL, blk, :], sp1[0:BL, :], 0.0, sp2[0:BL, :], ALU.max, ALU.add, ["b1_sp1", "b1_sp2"], [("dtp", blk)])
                        tt("dve", dAt[0:BL, blk, :], dtp[0:BL, blk, :], a_b[0:BL, :], ALU.mult, [("dtp", blk), "a_b"], [("dAt", blk)])
                    if is_p:
                        E("pool", "tensor_copy", ["knew"], [("kT", ti)], out=kT[:, :, tok0:tok0 + TW], in_=knew[:, :, 0:TW])
                        E("pool", "tensor_copy", ["kinew"], [("kiT", ti)], out=kiT[:, tok0:tok0 + TW], in_=kinew[:, 0:TW])
                    dbg("qT%d" % ti, qT[:, :, 0:TW], [128, QC, TW], "qT")
                    dbg("knew%d" % ti, knew[:, :, 0:TW], [128, NKV, TW], "knew")
                    dbg("qiT%d" % ti, qiT[:, :, 0:TW], [128, IC, TW], "qiT")
                    dbg("dtp%d" % ti, dtp[:], [128, NBLK, SH], "dtp")
                    dbg("wis%d" % ti, wis[:], [128, NBLK, IH], "wis")
                    P.barrier()
                    P.flush()
                stop = phase_done()

                with contextlib.ExitStack() as s2:
                  if not stop:
                    attention(ti, is_p, TW, tok0, knew, qiT, kinew, s2)
                    if ti == 0:
                        print("SBUF remaining in attention scope:", nc.sbuf_bytes_remaining)
                    dbg("oT%d" % ti, qT[:, :, 0:TW], [128, QC, TW], "qT")
                    P.barrier()
                    P.flush()
                    stop = phase_done()
            if stop:
                return True

            with contextlib.ExitStack() as s3:
                ssd_phase(ti, is_p, TW, tok0, segs, SL, s3)
                if ti == 0:
                    print("SBUF remaining in ssd scope:", nc.sbuf_bytes_remaining)
                dbg("y3T%d" % ti, y3T[:, :, 0:TW], [128, XC, TW], "y3T")
                P.barrier()
                P.flush()
            if phase_done():
                return True

            with contextlib.ExitStack() as s4:
                gsig = sb("gsig", [128, 2 * DC, T], BF16, s4)
                mixT = sb("mixT", [128, DC, T], BF16, s4)

                def gate_handler(cid, pp, pkey):
                    act(gsig[:, cid, 0:TW], pp, AF.Sigmoid, [pkey], [("gsig", cid)])
                dense_fm("gate", DC, lambda kc: hT[:, kc, 0:TW], TW, gate_handler, ["hT"])

                def ba_handler(cid, pp, pkey):
                    tt("dve", mixT[:, cid, 0:TW], pp, gsig[:, cid, 0:TW], ALU.mult, [pkey, ("gsig", cid)], [("mixT", cid)])
                dense_fm("ba", QC, lambda kc: qT[:, kc, 0:TW], TW, ba_handler, ["qT"])
                m2 = [sb("e_m2%d" % i, [128, T], F32, s4) for i in range(2)]
                rr2 = {"i": 0}

                def bs_handler(cid, pp, pkey):
                    b = rr2["i"] % 2
                    rr2["i"] += 1
                    tt("dve", m2[b][:, 0:TW], pp, gsig[:, DC + cid, 0:TW], ALU.mult, [pkey, ("gsig", DC + cid)], [("e_m2", b)])
                    tt("pool", mixT[:, cid, 0:TW], m2[b][:, 0:TW], mixT[:, cid, 0:TW], ALU.add, [("e_m2", b), ("mixT", cid)], [("mixT", cid)])
                dense_fm("bs", XC, lambda kc: y3T[:, kc, 0:TW], TW, bs_handler, ["y3T"])
                dbg("mixT%d" % ti, mixT[:, :, 0:TW], [128, DC, TW], "mixT")

                def out_handler(cid, pp, pkey):
                    for (s, t0, ln) in segs:
                        stt("dve", xT[:, cid, t0:t0 + ln], pp[:, t0:t0 + ln], GT1(cid, s), xT[:, cid, t0:t0 + ln], ALU.mult, ALU.add,
                            [pkey, "modT", ("xT", cid)], [("xT", cid)])
                dense_fm("out", DC, lambda kc: mixT[:, kc, 0:TW], TW, out_handler, ["mixT"])
                dbg("x1T%d" % ti, xT[:, :, 0:TW], [128, DC, TW], "xT")
                P.barrier()
                P.flush()
            if phase_done():
                return True

            with contextlib.ExitStack() as s5:
                rms_mod(xT, G2, SH2, hT, TW, segs, s5)
                aT = sb("aT", [128, FC, T], BF16, s5)
                sgb = [sb("f_sg%d" % i, [128, T], BF16, s5) for i in range(3)]
                yst = [sb("f_y%d" % i, [128, T], F32, s5) for i in range(2)]
                npair = max(len(p) for p in streams["gu"][1]) // 2

                def gu_handler(cid, pp, pkey):
                    j, isup = cid // 2, cid % 2
                    b = j % 3
                    if not isup:
                        act(sgb[b][:, 0:TW], pp, AF.Silu, [pkey], [("f_sg", b)])
                    else:
                        tt("dve", aT[:, j, 0:TW], pp, sgb[b][:, 0:TW], ALU.mult, [pkey, ("f_sg", b)], [("aT", j)])
                dense_fm("gu", DC, lambda kc: hT[:, kc, 0:TW], TW, gu_handler, ["hT"])
                rr3 = {"i": 0}

                def dn_handler(cid, pp, pkey):
                    b = rr3["i"] % 2
                    rr3["i"] += 1
                    for (s, t0, ln) in segs:
                        stt("dve", yst[b][:, t0:t0 + ln], pp[:, t0:t0 + ln], GT2(cid, s), xT[:, cid, t0:t0 + ln], ALU.mult, ALU.add,
                            [pkey, "modT", ("xT", cid)], [("f_y", b)])
                    P.dma("sp", lambda h: h.dma_start(out=o_y[:, cid, tok0:tok0 + TW], in_=yst[b][:, 0:TW]), reads=[("f_y", b)], is_output=True)
                dense_fm("dn", FC, lambda kc: aT[:, kc, 0:TW], TW, dn_handler, ["aT"])
                P.barrier()
                P.flush()
            return phase_done()

        vnew = sb("vnew", [128, NS, NKV * 64], BF16)

        def attention(ti, is_p, TW, tok0, knew, qiT, kinew, sc):
            LM = c.LMAX
            scores = [sb("at_score%d" % i, [128, LM], F32, sc) for i in range(2)]
            rts = [[sb("at_rt%d_%d" % (i, j), [128, 512], F32, sc) for j in range(2)] for i in range(2)]
            maskqs = [sb("at_maskq%d" % i, [128, LM], BF16, sc) for i in range(2)]
            maskTs = [sb("at_maskT%d" % i, [128, c.NBMAX, 128], BF16, sc) for i in range(2)]
            ep = [sb("at_ep%d" % i, [128, 512], BF16, sc) for i in range(2)]
            pp_ = [sb("at_pp%d" % i, [128, 512], BF16, sc) for i in range(2)]
            sms = [sb("at_sm%d" % i, [128, 64], F32, sc) for i in range(2)]
            steps2s = [sb("at_steps%d" % i, [128, 32], F32, sc) for i in range(2)]
            steps2ns = [sb("at_stepsn%d" % i, [128, 32], F32, sc) for i in range(2)]
            rden = [sb("at_rden%d" % i, [128, 256], F32, sc) for i in range(2)]
            NIT = c.NIT
            if is_p:
                qblocks = [(None, qb * 128, 128) for qb in range(TW // 128)]
            else:
                qblocks = [(s, s * DS, DS) for s in range(NS)]

            def geom(qi_):
                sidx, q0, QB = qblocks[qi_]
                if is_p:
                    gb = (tok0 + q0) // 128
                    L = (gb + 1) * 128
                    K = c.KP
                else:
                    L = PAST + DS
                    K = c.KS
                nkb = (L + 127) // 128
                kws = [min(128, L - kb * 128) for kb in range(nkb)]
                return sidx, q0, QB, L, K, nkb, kws

            def mask_stage(qi_):
                sidx, q0, QB, L, K, nkb, kws = geom(qi_)
                maskT = maskTs[qi_ % 2]
                mk = "at_maskT%d" % (qi_ % 2)
                par = qi_ % 2
                score, maskq, sm, steps2, steps2n, rt = scores[par], maskqs[par], sms[par], steps2s[par], steps2ns[par], rts[par]
                blk = qi_
                if not is_p:
                    s = sidx
                    DM("pool", kiT[0:64, 0:PAST], ckiT_d[s], writes=["kiT"])
                    DM("pool", kiT[64:128, 0:PAST], ckiT_d[s], writes=["kiT"])
                    E("pool", "tensor_copy", ["kinew"], ["kiT"], out=kiT[:, PAST:PAST + DS], in_=kinew[:, q0:q0 + DS])
                nkc = (L + 511) // 512
                ri = 0
                for kc in range(nkc):
                    w = min(512, L - kc * 512)
                    for h2 in range(0, IH, 2):
                        mmg([(ps[(4, 7)[hh % 2]][0:QB, 0:w], qiT[(hh % 2) * 64:(hh % 2) * 64 + 64, hh // 2, q0:q0 + QB],
                              kiT[(hh % 2) * 64:(hh % 2) * 64 + 64, kc * 512:kc * 512 + w], True, True, {}) for hh in (h2, h2 + 1)],
                            ["qiT", "kiT"], [("ps", 4), ("ps", 7)])
                        for hh in (h2, h2 + 1):
                            bank = (4, 7)[hh % 2]
                            if hh % 2 == 1:
                                ts("dve", rt[ri % 2][0:QB, 0:w], ps[bank][0:QB, 0:w], 0.0, wis[0:QB, blk, hh:hh + 1], ALU.max, ALU.mult,
                                   [("ps", bank), "wis"], [("at_rt%d" % par, ri % 2)])
                                tt("dve", score[0:QB, kc * 512:kc * 512 + w], rt[ri % 2][0:QB, 0:w], score[0:QB, kc * 512:kc * 512 + w], ALU.add,
                                   [("at_rt%d" % par, ri % 2), ("at_score%d" % par, kc)], [("at_score%d" % par, kc)])
                                ri += 1
                                continue
                            act(rt[ri % 2][0:QB, 0:w], ps[bank][0:QB, 0:w], AF.Relu, [("ps", bank)], [("at_rt%d" % par, ri % 2)])
                            if hh == 0:
                                ts("dve", score[0:QB, kc * 512:kc * 512 + w], rt[ri % 2][0:QB, 0:w], wis[0:QB, blk, hh:hh + 1], None, ALU.mult, None,
                                   [("at_rt%d" % par, ri % 2), "wis"], [("at_score%d" % par, kc)])
                            else:
                                stt("dve", score[0:QB, kc * 512:kc * 512 + w], rt[ri % 2][0:QB, 0:w], wis[0:QB, blk, hh:hh + 1],
                                    score[0:QB, kc * 512:kc * 512 + w], ALU.mult, ALU.add, [("at_rt%d" % par, ri % 2), "wis", ("at_score%d" % par, kc)],
                                    [("at_score%d" % par, kc)])
                            ri += 1
                        yield
                E("dve", "tensor_reduce", ["at_score%d" % par], [("at_sm%d" % par, 0)], out=sm[0:QB, 0:1], in_=score[0:QB, 0:L], axis=AX.X, op=ALU.max,
                  apply_absolute_value=True)
                ts("dve", sm[0:QB, 0:1], sm[0:QB, 0:1], 1.0001, 1e-20, ALU.mult, ALU.add, [("at_sm%d" % par, 0)], [("at_sm%d" % par, 0)])
                ts("dve", steps2[0:QB, 0:NIT + 1], cfv("pow2", 0, NIT + 1)[0:QB, :], sm[0:QB, 0:1], None, ALU.mult, None, ["cf", ("at_sm%d" % par, 0)], ["at_steps%d" % par])
                if is_p:
                    tt("dve", score[0:QB, L - 128:L], score[0:QB, L - 128:L], cfv("mb2"), ALU.add, ["at_score%d" % par, "cf"], ["at_score%d" % par])
                E("dve", "memset", [], [("at_sm%d" % par, 1)], sm[0:QB, 1:2], 0.0)
                yield
                thrK = float(2 * K - L)
                for it in range(1, NIT + 1):
                    if it % 4 == 0:
                        ts("dve", maskq[0:QB, 0:L], score[0:QB, 0:L], sm[0:QB, 1:2], None, ALU.is_ge, ALU.add, ["at_score%d" % par, ("at_sm%d" % par, 1)],
                           ["at_maskq%d" % par, ("at_sm%d" % par, 2)], accum_out=sm[0:QB, 2:3])
                        yield
                        ts("dve", sm[0:QB, 3:4], sm[0:QB, 2:3], float(K) - 0.5, 0.5, ALU.is_gt, ALU.subtract, [("at_sm%d" % par, 2)], [("at_sm%d" % par, 3)])
                    else:
                        act(maskq[0:QB, 0:L], score[0:QB, 0:L], AF.Sign, ["at_score%d" % par, ("at_sm%d" % par, 1)], ["at_maskq%d" % par, ("at_sm%d" % par, 2)],
                            scale=-1.0, bias=sm[0:QB, 1:2], accum_out=sm[0:QB, 2:3])
                        yield
                        ts("dve", sm[0:QB, 3:4], sm[0:QB, 2:3], -thrK + 0.5, 0.5, ALU.is_lt, ALU.subtract, [("at_sm%d" % par, 2)], [("at_sm%d" % par, 3)])
                    yield
                    stt("dve", sm[0:QB, 1:2], sm[0:QB, 3:4], steps2[0:QB, it:it + 1], sm[0:QB, 1:2], ALU.mult, ALU.add,
                        [("at_sm%d" % par, 3), "at_steps%d" % par, ("at_sm%d" % par, 1)], [("at_sm%d" % par, 1)])
                    yield
                stt("dve", sm[0:QB, 4:5], steps2[0:QB, NIT:NIT + 1], -0.5, sm[0:QB, 1:2], ALU.mult, ALU.add, ["at_steps%d" % par, ("at_sm%d" % par, 1)], [("at_sm%d" % par, 4)])
                yield
                ts("dve", maskq[0:QB, 0:L], score[0:QB, 0:L], sm[0:QB, 4:5], None, ALU.is_ge, None, ["at_score%d" % par, ("at_sm%d" % par, 4)], ["at_maskq%d" % par])
                yield
                for kb0 in range(0, nkb, 4):
                    nb_ = min(4, nkb - kb0)
                    for j in range(nb_):
                        kb = kb0 + j
                        E("pe", "transpose", ["at_maskq%d" % par, "cb"], [("ps", 7)], psb[0:kws[kb], j * 128: j * 128 + QB],
                          maskq[0:QB, kb * 128:kb * 128 + kws[kb]], cbv("ident")[0:QB, 0:QB])
                    for j in range(nb_):
                        kb = kb0 + j
                        act(maskT[0:kws[kb], kb, 0:QB], psb[0:kws[kb], j * 128: j * 128 + QB], AF.Copy,
                            [("ps", 7)], [(mk, kb)])
                    yield

            def attend_stage(qi_):
                sidx, q0, QB, L, K, nkb, kws = geom(qi_)
                maskT = maskTs[qi_ % 2]
                mk = "at_maskT%d" % (qi_ % 2)
                if not is_p:
                    s = sidx
                    for g_ in range(NKV):
                        DM("pool", kT[0:64, g_, 0:PAST], ckT_d[s][:, g_, :], writes=[("kT", g_)])
                        DM("pool", kT[64:128, g_, 0:PAST], ckT_d[s][:, g_, :], writes=[("kT", g_)])
                        E("pool", "tensor_copy", ["knew"], [("kT", g_)], out=kT[:, g_, PAST:PAST + DS], in_=knew[:, g_, q0:q0 + DS])
                    DM("pool", Vc[:, 0:PAST // 128, :], cv_d[s], writes=["Vc"])
                    E("pool", "tensor_copy", [("vnew", s)], ["Vc"], out=Vc[0:DS, PAST // 128, :], in_=vnew[0:DS, s, :])
                steps = [(g, kb) for g in range(NKV) for kb in range(nkb)]

                def qk(si):
                    g, kb = steps[si]
                    kw = kws[kb]
                    sbk = si % 2
                    mmg([(ps[(sbk, 5 + sbk)[a]][0:kw, 0:2 * QB], kT[a * 64:(a + 1) * 64, g, kb * 128:kb * 128 + kw],
                          qT[a * 64:(a + 1) * 64, 2 * g:2 * g + 2, q0:q0 + QB], True, True, {}) for a in range(2)],
                        [("kT", g), ("qT", 2 * g), ("qT", 2 * g + 1)], [("ps", sbk), ("ps", 5 + sbk)])

                qk(0)
                for si, (g, kb) in enumerate(steps):
                    ob = 2 + g % 2
                    kw = kws[kb]
                    sbk = si % 2
                    if si + 1 < len(steps):
                        qk(si + 1)
                    for a in range(2):
                        sb_a = (sbk, 5 + sbk)[a]
                        act(ep[sbk][0:kw, a * 2 * QB:(a + 1) * 2 * QB], ps[sb_a][0:kw, 0:2 * QB], AF.Exp, [("ps", sb_a)], [("at_ep", sbk, a)], scale=0.125)
                    tt("dve", pp_[sbk][0:kw, 0:4 * QB].rearrange("p (a t) -> p a t", a=4), ep[sbk][0:kw, 0:4 * QB].rearrange("p (a t) -> p a t", a=4),
                       bc_mid(maskT[0:kw, kb, 0:QB], 4), ALU.mult, [("at_ep", sbk), (mk, kb)], [("at_pp", sbk)])
                    items = []
                    for a in range(2):
                        kw_ = dict(tile_position=(0, a * 64), skip_group_check=True)
                        items.append((ps[ob][a * 64:(a + 1) * 64, 0:2 * QB], Vc[0:kw, kb, g * 64:(g + 1) * 64], pp_[sbk][0:kw, a * 2 * QB:(a + 1) * 2 * QB],
                                      kb == 0, kb == nkb - 1, kw_))
                        items.append((ps[ob][a * 64:(a + 1) * 64, 256:256 + 2 * QB], cbv("ones")[0:kw, 0:64], pp_[sbk][0:kw, a * 2 * QB:(a + 1) * 2 * QB],
                                      False, kb == nkb - 1, kw_))
                    mmg(items, [("Vc", g), "cb", ("at_pp", sbk)], [("ps", ob)])
                    yield
                    if kb == nkb - 1:
                        E("dve", "reciprocal", [("ps", ob)], [("at_rden", g % 2)], out=rden[g % 2][:, 0:2 * QB], in_=ps[ob][:, 256:256 + 2 * QB])
                        tt("dve", qT[:, 2 * g:2 * g + 2, q0:q0 + QB], ps[ob][:, 0:2 * QB].rearrange("p (e t) -> p e t", e=2),
                           rden[g % 2][:, 0:2 * QB].rearrange("p (e t) -> p e t", e=2), ALU.mult, [("ps", ob), ("at_rden", g % 2)],
                           [("qT", 2 * g, qi_), ("qT", 2 * g + 1, qi_)])
                        yield

            def drain2(ga, gb_):
                gens = [g_ for g_ in (ga, gb_) if g_ is not None]
                while gens:
                    for g_ in list(gens):
                        try:
                            next(g_)
                        except StopIteration:
                            gens.remove(g_)

            prev = None
            for qi_ in range(len(qblocks)):
                drain2(mask_stage(qi_), prev)
                prev = attend_stage(qi_)
            drain2(prev, None)

        def ssd_phase(ti, is_p, TW, tok0, segs, SL, sc):
            NSEG = len(segs)
            Lc = 128 if is_p else DS
            xraw = [sb("sd_xraw%d" % i, [128, 4, NSEG, 3 + SL], BF16, sc) for i in range(2)]
            zs = [sb("sd_zs%d" % i, [128, 2, T], BF16, sc) for i in range(2)]
            acc = [sb("sd_acc%d" % i, [128, T], F32, sc) for i in range(2)]
            xbc = [sb("sd_xbc%d" % i, [128, 4, T], BF16, sc) for i in range(2)]
            y1 = [sb("sd_y1%d" % i, [128, 2, T], BF16, sc) for i in range(2)]
            def two(name, shape, dt):
                return [sb("%s%d" % (name, i), shape, dt, sc) for i in range(2)]
            Zt2 = two("sd_Z", [128, 4, 128], F32)
            W22 = two("sd_W2", [128, 4, 128], F32)
            acs2 = two("sd_acs", [128, 4], F32)
            E1b2 = two("sd_E1b", [128, 4, 128], BF16)
            Em2 = two("sd_E", [128, 4, 128], BF16)
            LT2 = two("sd_LT", [128, 4, 128], BF16)
            CpT2 = two("sd_CpT", [128, 4, 128], BF16)
            cbs2 = two("sd_cbs", [128, 128], BF16)
            Xdt2 = two("sd_Xdt", [128, 4, 64], BF16)
            Xw2 = two("sd_Xw", [128, 4, 64], BF16)
            Btok2 = two("sd_Btok", [128, 128], BF16)
            wsm2 = two("sd_ws", [128, 4], F32)
            CDb2 = two("sd_CDb", [128, 4], F32)
            ckc = {"n": 0}
            NSL = 1 if is_p else NS
            hs4 = sb("sd_hs", [128, NSL, 256], F32, sc)
            hsb4 = sb("sd_hsb", [128, NSL, 256], BF16, sc)
            stt_ = sb("sd_sttmp", [128, 256], F32, sc)
            y2 = [sb("sd_y2%d" % i, [128, T], BF16, sc) for i in range(2)]
            sq1 = sb("sd_sq", [128, T], BF16, sc)
            sq = [sq1, sq1]
            rs1 = sb("sd_rs1", [128, T], F32, sc)
            rstd = sb("sd_rstd", [128, T], F32, sc)
            if not is_p:
                scv = sb("sd_sconv", [128, NXB, NS, 3], BF16, sc)
                P.dma("pool", lambda h: h.dma_start(out=scv[:], in_=sconv_d), writes=["sd_sconv"])
            _, pieces = streams["b2"]
            def proj_stage(g):
                gb = g % 2
                wv, wkey = w_get("b2", g)
                xids = [2 * g, 2 * g + 1, XC + g, XC + SG + g]
                for j in range(6):
                    bank = dense_rr["i"] % 2
                    dense_rr["i"] += 1
                    for kc in range(DC):
                        mm(ps[bank][:, 0:TW], wv[:, kc, j * 128:(j + 1) * 128], hT[:, kc, 0:TW], kc == 0, kc == DC - 1, [wkey, "hT"], [("ps", bank)])
                    if j < 4:
                        xc = xids[j]
                        for si_, (s, t0, ln) in enumerate(segs):
                            act(xraw[gb][:, j, si_, 3:3 + ln], ps[bank][:, t0:t0 + ln], AF.Copy, [("ps", bank)], [("sd_xraw", gb, j)])
                            if is_p:
                                E("pool", "tensor_copy", [("chist", xc)], [("sd_xraw", gb, j)], out=xraw[gb][:, j, si_, 0:3], in_=chist[:, xc, :])
                            else:
                                E("pool", "tensor_copy", ["sd_sconv"], [("sd_xraw", gb, j)], out=xraw[gb][:, j, si_, 0:3], in_=scv[:, xc, si_, :])
                        if is_p:
                            E("pool", "tensor_copy", [("sd_xraw", gb, j)], [("chist", xc)], out=chist[:, xc, :], in_=xraw[gb][:, j, 0, SL:SL + 3])
                            if ti == c.NT - 1:
                                E("pool", "tensor_copy", [("sd_xraw", gb, j)], [("cst", xc)], out=cst[:, xc, 0, :], in_=xraw[gb][:, j, 0, SL:SL + 3])
                        else:
                            E("pool", "tensor_copy", [("sd_xraw", gb, j)], [("cst", xc)], out=cst[:, xc, 1:1 + NS, :], in_=xraw[gb][:, j, :, SL:SL + 3])
                        ab = j % 2
                        cw = lambda k: vv("convw", xc * 4 + k, xc * 4 + k + 1)
                        av = acc[ab][:, 0:TW].rearrange("p (s l) -> p s l", s=NSEG)
                        ts("dve", av, xraw[gb][:, j, :, 0:SL], cw(0), vv("convb", xc, xc + 1), ALU.mult, ALU.add, [("sd_xraw", gb, j), "vec"], [("sd_acc", ab)])
                        for k in range(1, 4):
                            stt("dve", av, xraw[gb][:, j, :, k:k + SL], cw(k), av, ALU.mult, ALU.add, [("sd_xraw", gb, j), "vec", ("sd_acc", ab)], [("sd_acc", ab)])
                        act(xbc[gb][:, j, 0:TW], acc[ab][:, 0:TW], AF.Silu, [("sd_acc", ab)], [("sd_xbc", gb, j)])
                    else:
                        act(zs[gb][:, j - 4, 0:TW], ps[bank][:, 0:TW], AF.Silu, [("ps", bank)], [("sd_zs", gb, j - 4)])
                    yield
                if "xbc" in debug and g == 0:
                    dbg("xbc%d" % ti, xbc[gb][:, :, 0:TW], [128, 4, TW], ("sd_xbc", gb))
                yield

            def scan_stage(g):
                gb = g % 2
                if not is_p:
                    DM("sp", hs4[:, :, :], sssm_d[:, g].rearrange("s n f -> n s f"), writes=["sd_hs"])
                for si_, (s, t0s, ln) in enumerate(segs):
                    if is_p:
                        hcur = hst[:, g, :]
                        hkey = ("hst", g)
                    else:
                        hcur = hs4[:, si_, :]
                        hkey = ("sd_hs", si_)
                    hsb = hsb4[:, si_ if not is_p else 0, :]
                    hbk = ("sd_hsb", si_ if not is_p else 0)
                    E("act", "copy", [hkey], [hbk], out=hsb, in_=hcur)
                    for ck in range(ln // Lc):
                        t0 = t0s + ck * Lc
                        blk = t0 // (128 if is_p else DS)
                        pk = ckc["n"] % 2
                        ckc["n"] += 1
                        Zt, W2, acs, E1b, Em, LT, CpT, cbs = Zt2[pk], W22[pk], acs2[pk], E1b2[pk], Em2[pk], LT2[pk], CpT2[pk], cbs2[pk]
                        Xdt, Xw, Btok, wsm, CDb = Xdt2[pk], Xw2[pk], Btok2[pk], wsm2[pk], CDb2[pk]
                        K_ = lambda n: (n, pk)
                        dA = dAt[0:Lc, blk, 4 * g:4 * g + 4]
                        dtv = dtp[0:Lc, blk, 4 * g:4 * g + 4]
                        X0 = xbc[gb][:, 0, t0:t0 + Lc]
                        X1 = xbc[gb][:, 1, t0:t0 + Lc]
                        Bf = xbc[gb][:, 2, t0:t0 + Lc]
                        Cf = xbc[gb][:, 3, t0:t0 + Lc]
                        xk = [("sd_xbc", gb, jj) for jj in range(4)]
                        mm(ps[6][0:Lc, 0:4], cfv("tri")[0:Lc, 0:Lc], dA, True, True, ["cf", "dAt"], [("ps", 6)])
                        act(acs[0:Lc, :], ps[6][0:Lc, 0:4], AF.Copy, [("ps", 6)], [K_("sd_acs")])
                        tt("pool", Zt[0:Lc, :, 0:Lc], bc_last(dA, Lc), bc_mid(cfv("tri")[0:Lc, 0:Lc], 4), ALU.mult, ["dAt", "cf"], [K_("sd_Z")])
                        tt("pool", W2[0:Lc, :, 0:Lc], bc_mid(cfv("mb")[0:Lc, 0:Lc], 4), bc_last(acs[0:Lc, :], Lc), ALU.subtract, ["cf", K_("sd_acs")], [K_("sd_W2")])
                        mm(ps[4][:, 0:4 * Lc], cfv("ones")[0:Lc, :], Zt[0:Lc, :, 0:Lc], True, True, ["cf", K_("sd_Z")], [("ps", 4)])
                        mm(ps[5][0:Lc, 0:4 * Lc], cfv("ones")[0:Lc, 0:Lc], Zt[0:Lc, :, 0:Lc], True, False, ["cf", K_("sd_Z")], [("ps", 5)])
                        mm(ps[5][0:Lc, 0:4 * Lc], cfv("ident")[0:Lc, 0:Lc], W2[0:Lc, :, 0:Lc], False, True, ["cf", K_("sd_W2")], [("ps", 5)])
                        p4 = ps[4][:, 0:4 * Lc].rearrange("p (h i) -> p h i", h=4)
                        p5 = ps[5][0:Lc, 0:4 * Lc].rearrange("p (h i) -> p h i", h=4)
                        act(E1b[:, :, 0:Lc], p4, AF.Exp, [("ps", 4)], [K_("sd_E1b")])
                        act(CDb[:, :], p4[:, :, Lc - 1], AF.Exp, [("ps", 4)], [K_("sd_CDb")])
                        act(Em[0:Lc, :, 0:Lc], p5, AF.Exp, [("ps", 5)], [K_("sd_E")])
                        act(wsm[0:Lc, :], p5[:, :, Lc - 1], AF.Exp, [("ps", 5)], [K_("sd_ws")])
                        mm(ps[6][0:Lc, 128:128 + Lc], Bf, Cf, True, True, [xk[2], xk[3]], [("ps", 6)])
                        act(cbs[0:Lc, 0:Lc], ps[6][0:Lc, 128:128 + Lc], AF.Copy, [("ps", 6)], [K_("sd_cbs")])
                        tt("dve", LT[0:Lc, :, 0:Lc], Em[0:Lc, :, 0:Lc], bc_mid(cbs[0:Lc, 0:Lc], 4), ALU.mult, [K_("sd_E"), K_("sd_cbs")], [K_("sd_LT")])
                        tt("dve", CpT[:, :, 0:Lc], E1b[:, :, 0:Lc], bc_mid(Cf, 4), ALU.mult, [K_("sd_E1b"), xk[3]], [K_("sd_CpT")])
                        E("pe", "transpose", [xk[0], "cb"], [("ps", 7)], psb[0:Lc, 0:128], X0, cbv("ident"))
                        E("pe", "transpose", [xk[1], "cb"], [("ps", 7)], psb[0:Lc, 128:256], X1, cbv("ident"))
                        E("pe", "transpose", [xk[2], "cb"], [("ps", 7)], psb[0:Lc, 256:384], Bf, cbv("ident"))
                        tt("dve", Xdt[0:Lc, :, :], psb[0:Lc, 0:256].rearrange("p (h d) -> p h d", h=4), bc_last(dtv, 64), ALU.mult, [("ps", 7), "dtp"], [K_("sd_Xdt")])
                        E("dve", "tensor_copy", [("ps", 7)], [K_("sd_Btok")], out=Btok[0:Lc, :], in_=psb[0:Lc, 256:384])
                        yield
                        items = []
                        for e in range(2):
                            for a in range(2):
                                lh = 2 * e + a
                                kw_ = dict(tile_position=(0, a * 64), skip_group_check=True)
                                items.append((ps[2][a * 64:(a + 1) * 64, e * 128:e * 128 + Lc], Xdt[0:Lc, lh, :], LT[0:Lc, lh, 0:Lc], True, False, kw_))
                                items.append((ps[2][a * 64:(a + 1) * 64, e * 128:e * 128 + Lc], hsb[:, lh * 64:(lh + 1) * 64], CpT[:, lh, 0:Lc], False, True, kw_))
                        mmg(items, [K_("sd_Xdt"), K_("sd_LT"), hbk, K_("sd_CpT")], [("ps", 2)])
                        for e in range(2):
                            XTe = xbc[gb][:, e, t0:t0 + Lc]
                            stt("dve", y1[gb][:, e, t0:t0 + Lc], XTe, vv("dskip", 2 * g + e, 2 * g + e + 1), ps[2][:, e * 128:e * 128 + Lc], ALU.mult, ALU.add,
                                [xk[e], "vec", ("ps", 2)], [("sd_y1", gb, e)])
                        tt("dve", Xw[0:Lc, :, :], Xdt[0:Lc, :, :], bc_last(wsm[0:Lc, :], 64), ALU.mult, [K_("sd_Xdt"), K_("sd_ws")], [K_("sd_Xw")])
                        mm(ps[3][:, 0:256], Btok[0:Lc, :], Xw[0:Lc, :, :], True, True, [K_("sd_Btok"), K_("sd_Xw")], [("ps", 3)])
                        tt("pool", stt_[:, :].rearrange("p (h d) -> p h d", h=4), hcur.rearrange("p (h d) -> p h d", h=4), bc_last(CDb[:, :], 64), ALU.mult,
                           [hkey, K_("sd_CDb")], ["sd_sttmp"])
                        tt("dve", hcur, stt_[:, :], ps[3][:, 0:256], ALU.add, ["sd_sttmp", ("ps", 3)], [hkey])
                        E("act", "copy", [hkey], [hbk], out=hsb, in_=hcur)
                        yield
                    if is_p and ti == c.NT - 1:
                        DM("sp", o_ssm[0, g], hst[:, g, :], reads=[("hst", g)], is_output=True)
                if not is_p:
                    DM("sp", o_ssm[1:1 + NS, g].rearrange("s n f -> n s f"), hs4[:, :, :], reads=["sd_hs"], is_output=True)
                for e in range(2):
                    tt("dve", y2[e][:, 0:TW], y1[gb][:, e, 0:TW], zs[gb][:, e, 0:TW], ALU.mult, [("sd_y1", gb, e), ("sd_zs", gb, e)], [("sd_y2", e)])
                    act(sq[e][:, 0:TW], y2[e][:, 0:TW], AF.Square, [("sd_y2", e)], ["sd_sq"])
                    mm(ps[3][:, 0:TW], cbv("ones"), sq[e][:, 0:TW], e == 0, e == 1, ["sd_sq", "cb"], [("ps", 3)])
                act(rs1[:, 0:TW], ps[3][:, 0:TW], AF.Ln, [("ps", 3)], ["sd_rs1"], scale=1.0 / 256, bias=epsc[:, 0:1])
                act(rstd[:, 0:TW], rs1[:, 0:TW], AF.Exp, ["sd_rs1"], ["sd_rstd"], scale=-0.5)
                for e in range(2):
                    stt("dve", y3T[:, 2 * g + e, 0:TW], y2[e][:, 0:TW], vv("gssm", 2 * g + e, 2 * g + e + 1), rstd[:, 0:TW], ALU.mult, ALU.mult,
                        [("sd_y2", e), "vec", "sd_rstd"], [("y3T", 2 * g + e)])

                yield

            def drain2(ga, gb_):
                gens = [g_ for g_ in (ga, gb_) if g_ is not None]
                while gens:
                    for g_ in list(gens):
                        try:
                            next(g_)
                        except StopIteration:
                            gens.remove(g_)

            drain2(proj_stage(0), None)
            for g in range(SG):
                drain2(scan_stage(g), proj_stage(g + 1) if g + 1 < SG else None)

        if limit is None or limit > 0:
            import os as _os
            tiles = [int(x) for x in _os.environ["TILES"].split(",")] if "TILES" in _os.environ else list(range(c.NT + 1))
            for ti in tiles:
                if ti != tiles[0] or ti == 0:
                    pass
                if do_tile(ti):
                    break
        P.dma("sp", lambda h: h.dma_start(out=o_conv, in_=cst[:]), reads=["cst"], is_output=True)
        P.barrier()
        P.flush(final=True)
    return nc, P


def _chunked(v, n):
    return np.ascontiguousarray(np.asarray(v, np.float32).reshape(n, 128).T)


def _stream(W, pieces, KC):
    mx = max(len(p) for p in pieces) * 128 * KC
    out = np.zeros((len(pieces), 128, mx), np.float32)
    for pi, cols in enumerate(pieces):
        idx = np.concatenate(cols)
        blk = W[:, idx].reshape(KC, 128, len(idx)).transpose(1, 0, 2).reshape(128, KC * len(idx))
        out[pi, :, :blk.shape[1]] = blk
    return out


def host_consts(c):
    cf = np.zeros((128, c.NF), np.float32)
    k = np.arange(128)
    def put(name, a):
        o, w = c.cf_off[name]
        cf[:, o:o + a.shape[1]] = a
    put("tri", (k[:, None] <= k[None, :]).astype(np.float32))
    put("ones", np.ones((128, 128), np.float32))
    put("ident", np.eye(128, dtype=np.float32))
    put("mb", np.where(k[:, None] > k[None, :], -1.0e5, 0.0).astype(np.float32))
    put("mb2", np.where((k[:, None] < 64) & (k[None, :] >= 64), -1.0e30, 0.0).astype(np.float32))
    p2 = np.zeros((128, 32), np.float32)
    for i in range(1, 32):
        p2[:, i] = 2.0 ** -(i - 1)
    put("pow2", p2)
    cb = np.zeros((128, c.NB), np.float32)
    def putb(name, a):
        o, w = c.cb_off[name]
        cb[:, o:o + w] = a
    putb("ident", np.eye(128, dtype=np.float32))
    putb("ones", np.ones((128, 128), np.float32))
    putb("blk", ((k[:, None] // 64) == (k[None, :] // 64)).astype(np.float32))
    rot = np.zeros((128, 128), np.float32)
    for m in range(128):
        if (m % 64) < 32:
            rot[m + 32, m] = -1.0
        else:
            rot[m - 32, m] = 1.0
    putb("rot", rot)
    return cf, cb


def host_weights(c, inp):
    st = c.streams()
    D, DC = c.D, c.DC
    w_in = np.asarray(inp["w_in"][0], np.float32)
    sizes = [c.NH * 64, c.NKV * 64, c.NKV * 64, c.IH * 64, 64, c.IH, c.DI, c.DI + 2 * c.SG * 128, c.SH, 2 * D]
    offs = np.concatenate([[0], np.cumsum(sizes)])
    oq, ok, ov, oqi, oki, owi, oz, oxbc, odt, og = offs[:10]
    ar = np.arange
    out = {}
    ch = lambda base, i: base + i * 128 + ar(128)
    b1 = [ch(oq, i) for i in range(c.QC)]
    b1 += [np.concatenate([ok + g * 64 + ar(64)] * 2) for g in range(c.NKV)]
    b1 += [ch(oqi, i) for i in range(c.IC)]
    b1 += [np.concatenate([oki + ar(64)] * 2)]
    out["w_b1"] = _stream(w_in, [[b1[i] for i in p] for p in st["b1"][1]], DC)
    b2 = []
    for g in range(c.SG):
        b2.append([ch(oxbc, 2 * g), ch(oxbc, 2 * g + 1), ch(oxbc, c.XC + g), ch(oxbc, c.XC + c.SG + g), ch(oz, 2 * g), ch(oz, 2 * g + 1)])
    out["w_b2"] = _stream(w_in, b2, DC)
    gate = [ch(og, i) for i in range(2 * DC)]
    out["w_gate"] = _stream(w_in, [[gate[i] for i in p] for p in st["gate"][1]], DC)
    tokc = np.concatenate([ov + ar(c.NKV * 64), owi + ar(c.IH), odt + ar(c.SH)])
    out["w_tok"] = np.ascontiguousarray(w_in[:, tokc].reshape(DC, 128, c.NTOK).transpose(1, 0, 2).reshape(128, DC * c.NTOK))
    def plain(W, name, KC):
        cols = [i * 128 + ar(128) for i in range(W.shape[1] // 128)]
        return _stream(np.asarray(W, np.float32), [[cols[i] for i in p] for p in st[name][1]], KC)
    out["w_ada"] = plain(inp["w_ada"][0], "ada", DC)
    out["w_ba"] = plain(inp["w_branch_attn"][0], "ba", c.QC)
    out["w_bs"] = plain(inp["w_branch_ssm"][0], "bs", c.XC)
    out["w_out"] = plain(inp["w_out"][0], "out", DC)
    wgu = np.asarray(inp["w_gate_up"][0], np.float32)
    gu = []
    for j in range(c.FC):
        gu.append(j * 128 + ar(128))
        gu.append(c.FF + j * 128 + ar(128))
    out["w_gu"] = _stream(wgu, [[gu[i] for i in p] for p in st["gu"][1]], DC)
    out["w_dn"] = plain(inp["w_down"][0], "dn", c.FC)
    vec = np.zeros((128, c.NV), np.float32)
    def put(name, a):
        o, w = c.vec_off[name]
        assert a.shape == (128, w), (name, a.shape, w)
        vec[:, o:o + w] = a
    put("gmix", _chunked(inp["g_norm_mix"][0], DC))
    put("gffn", _chunked(inp["g_norm_ffn"][0], DC))
    put("bada", _chunked(inp["b_ada"][0], 6 * DC))
    p64 = ar(128) % 64
    put("gq", np.asarray(inp["g_q"][0], np.float32)[p64][:, None])
    put("gk", np.asarray(inp["g_k"][0], np.float32)[p64][:, None])
    wc = np.asarray(inp["w_conv"][0], np.float32)
    put("convw", np.ascontiguousarray(wc.T.reshape(c.NXB, 128, 4).transpose(1, 0, 2).reshape(128, c.NXB * 4)))
    put("convb", _chunked(inp["b_conv"][0], c.NXB))
    hd = (ar(c.XC)[None, :] * 2 + (ar(128) // 64)[:, None])
    put("dskip", np.asarray(inp["d_skip"][0], np.float32)[hd])
    put("gssm", _chunked(inp["g_ssm_norm"][0], c.XC))
    put("dtb", np.broadcast_to(np.asarray(inp["dt_bias"][0], np.float32)[None, :], (128, c.SH)).copy())
    put("alog", np.broadcast_to(np.asarray(inp["a_log"][0], np.float32)[None, :], (128, c.SH)).copy())
    out["vecs"] = vec
    cf, cb = host_consts(c)
    out["cf32"] = cf
    out["cbf"] = cb
    return out


def host_core_inputs(c, inp, r, shared):
    m = dict(shared)
    NS, DS, PAST, SEQ, D, DC = c.NS, c.DS, c.PAST, c.SEQ, c.D, c.DC
    xs = np.concatenate([np.asarray(inp["x_prompt"][r], np.float32)] +
                        [np.asarray(inp["x_sample"][NS * r + s], np.float32) for s in range(NS)], axis=0)
    m["xT"] = np.ascontiguousarray(xs.reshape(c.TT, DC, 128).transpose(2, 1, 0))
    cs = np.concatenate([np.asarray(inp["c_prompt"][r:r + 1], np.float32), np.asarray(inp["c_sample"][NS * r:NS * r + NS], np.float32)], axis=0)
    m["cT"] = np.ascontiguousarray(cs.reshape(c.NSEQ, DC, 128).transpose(2, 1, 0))
    pos = np.concatenate([np.arange(SEQ)] + [PAST + np.arange(DS)] * NS).astype(np.float32)
    inv = (c.THETA ** (-np.arange(32, dtype=np.float32) / 32)).astype(np.float32)
    ang = pos[None, :] * inv[np.arange(128) % 32][:, None]
    m["rope"] = np.ascontiguousarray(np.stack([np.cos(ang), np.sin(ang)], axis=1).astype(np.float32))
    sl = slice(NS * r, NS * r + NS)
    ck = np.asarray(inp["cache_k"][0][sl], np.float32)
    m["ckT"] = np.ascontiguousarray(ck.transpose(0, 3, 2, 1))
    m["ckiT"] = np.ascontiguousarray(np.asarray(inp["cache_ki"][0][sl], np.float32).transpose(0, 2, 1))
    cv = np.asarray(inp["cache_v"][0][sl], np.float32).reshape(NS, PAST // 128, 128, c.NKV * 64)
    m["cv"] = np.ascontiguousarray(cv.transpose(0, 2, 1, 3))
    scv = np.asarray(inp["state_conv"][0][sl], np.float32)
    m["sconv"] = np.ascontiguousarray(scv.reshape(NS, 3, c.NXB, 128).transpose(3, 2, 0, 1))
    ss = np.asarray(inp["state_ssm"][0][sl], np.float32)
    m["sssm"] = np.ascontiguousarray(ss.reshape(NS, c.SG, 4, 64, 128).transpose(0, 1, 4, 2, 3).reshape(NS, c.SG, 128, 256))
    return m


def host_assemble(c, results, nb):
    NS, DS, SEQ, D, DC = c.NS, c.DS, c.SEQ, c.D, c.DC
    CH = c.NXB * 128
    yp, ys, kp, vp, kip, cp, sp_, ks, vs, kis, cs_, ss = ([] for _ in range(12))
    for r in range(nb):
        o = results[r]
        y = np.asarray(o["o_y"]).transpose(2, 1, 0).reshape(c.TT, D)
        k = np.asarray(o["o_k"]).transpose(2, 1, 0)
        ki = np.asarray(o["o_ki"]).T
        v = np.asarray(o["o_v"]).reshape(c.TT, c.NKV, 64)
        cv = np.asarray(o["o_conv"]).transpose(2, 3, 1, 0).reshape(c.NSEQ, 3, CH)
        sm = np.asarray(o["o_ssm"]).reshape(c.NSEQ, c.SG, 128, 4, 64).transpose(0, 1, 3, 4, 2).reshape(c.NSEQ, c.SH, 64, 128)
        yp.append(y[:SEQ]); kp.append(k[:SEQ]); vp.append(v[:SEQ]); kip.append(ki[:SEQ]); cp.append(cv[0]); sp_.append(sm[0])
        for s in range(NS):
            a, b = SEQ + s * DS, SEQ + (s + 1) * DS
            ys.append(y[a:b]); ks.append(k[a:b]); vs.append(v[a:b]); kis.append(ki[a:b]); cs_.append(cv[1 + s]); ss.append(sm[1 + s])
    f = lambda l, lead: np.ascontiguousarray(np.stack(l).astype(np.float32))[None] if lead else np.ascontiguousarray(np.stack(l).astype(np.float32))
    return (f(yp, 0), f(ys, 0), f(kp, 1), f(vp, 1), f(kip, 1), f(cp, 1), f(sp_, 1), f(ks, 1), f(vs, 1), f(kis, 1), f(cs_, 1), f(ss, 1))


_CACHE = {}


def kernel(**inputs):
    c = FULL
    nb = 8
    if "nc" not in _CACHE:
        _CACHE["nc"] = build(c)[0]
    nc = _CACHE["nc"]
    shared = host_weights(c, inputs)
    in_maps = [host_core_inputs(c, inputs, r, shared) for r in range(nb)]
    res = run_bass_kernel_spmd(nc, in_maps, core_ids=list(range(nb)))
    return host_assemble(c, res.results, nb)
```

```python
import contextlib
import math
import numpy as np
import concourse.bass as bass
import concourse.mybir as mybir
from concourse.bass_utils import run_bass_kernel_spmd

F32 = mybir.dt.float32
BF16 = mybir.dt.bfloat16
ALU = mybir.AluOpType
AF = mybir.ActivationFunctionType
AX = mybir.AxisListType

ENGS = ("pe", "act", "dve", "pool", "sp")
EPS = 1e-6
WBUF_ELEMS = 6144


class Cfg:
    def __init__(self, D=1024, NH=16, NKV=4, IH=8, DI=2048, SG=8, FF=2816, SEQ=2048, T=512, NS=4, DS=64,
                 PAST=2048, NIT=16, TOPK_MAX=256, THETA=10000.0):
        self.D, self.NH, self.NKV, self.IH, self.DI, self.SG, self.FF = D, NH, NKV, IH, DI, SG, FF
        self.SEQ, self.T, self.NS, self.DS, self.PAST, self.NIT = SEQ, T, NS, DS, PAST, NIT
        self.THETA = THETA
        self.DC = D // 128
        self.QC = NH // 2
        self.IC = IH // 2
        self.XC = DI // 128
        self.SH = DI // 64
        self.FC = FF // 128
        self.NSEQ = 1 + NS
        self.TT = SEQ + NS * DS
        self.NT = SEQ // T
        self.KP = min(TOPK_MAX, SEQ // 4)
        self.KS = min(TOPK_MAX, (PAST + DS) // 4)
        self.NTOK = NKV * 64 + IH + self.SH
        self.NXB = self.XC + 2 * SG
        self.LMAX = max(SEQ, PAST + DS)
        self.NBMAX = (self.LMAX + 127) // 128
        assert NH == 4 * NKV and self.SH == 4 * SG and IH % 2 == 0 and SEQ % T == 0 and T % 128 == 0
        assert PAST % 128 == 0 and DS == 64 and FF % 128 == 0
        o = {}
        n = 0
        for name, w in (("gmix", self.DC), ("gffn", self.DC), ("bada", 6 * self.DC), ("gq", 1), ("gk", 1),
                        ("convw", self.NXB * 4), ("convb", self.NXB), ("dskip", self.XC), ("gssm", self.XC),
                        ("dtb", self.SH), ("alog", self.SH)):
            o[name] = (n, w)
            n += w
        self.vec_off, self.NV = o, n
        o = {}
        n = 0
        for name, w in (("tri", 128), ("ones", 128), ("ident", 128), ("mb", 128), ("mb2", 128), ("pow2", 32)):
            o[name] = (n, w)
            n += w
        self.cf_off, self.NF = o, n
        o = {}
        n = 0
        for name, w in (("ident", 128), ("ones", 128), ("blk", 128), ("rot", 128)):
            o[name] = (n, w)
            n += w
        self.cb_off, self.NB = o, n

    def streams(self):
        c = self
        def grp(ids, k):
            return [ids[i:i + k] for i in range(0, len(ids), k)]
        s = {}
        s["ada"] = (c.DC, grp(list(range(6 * c.DC)), 6))
        s["b1"] = (c.DC, grp(list(range(c.QC + c.NKV + c.IC + 1)), 6))
        s["b2"] = (c.DC, [list(range(6 * g, 6 * g + 6)) for g in range(c.SG)])
        s["gate"] = (c.DC, grp(list(range(2 * c.DC)), 6))
        s["ba"] = (c.QC, grp(list(range(c.DC)), max(1, min(6, WBUF_ELEMS // (c.QC * 128)))))
        s["bs"] = (c.XC, grp(list(range(c.DC)), max(1, min(6, WBUF_ELEMS // (c.XC * 128)))))
        s["out"] = (c.DC, grp(list(range(c.DC)), 6))
        npair = max(1, min(3, WBUF_ELEMS // (c.DC * 256)))
        s["gu"] = (c.DC, [sum([[2 * j for j in p], [2 * j + 1 for j in p]], []) for p in grp(list(range(c.FC)), npair)])
        s["dn"] = (c.FC, grp(list(range(c.DC)), max(1, min(6, WBUF_ELEMS // (c.FC * 128)))))
        return s


FULL = Cfg()


class Prog:
    LAT = 0.25
    DMA_FIX = 2.0
    DMA_BW = 150e3

    def __init__(self, nc, n_dma_sems=32, schedule=True):
        self.nc = nc
        self.schedule = schedule
        self.seg = []
        self.count = {e: 0 for e in ENGS}
        self.known = {e: {} for e in ENGS}
        self.res = {}
        self.n_dma_sems = n_dma_sems
        self.dma_val = [0] * n_dma_sems
        self.dma_last = [None] * n_dma_sems
        self.dma_rr = 0
        self.dma_rr_pool = 0
        self.out_sigs = []
        self.ninstr = 0
        self.nid = 0
        self.pre = {e: [] for e in ENGS}

    @staticmethod
    def _overlap(a, b):
        n = min(len(a), len(b))
        return a[:n] == b[:n]

    @staticmethod
    def _norm(keys):
        out = []
        for k in keys or []:
            if isinstance(k, str):
                k = (k,)
            out.append(tuple(k))
        return out

    def _deps_for(self, eng, reads, writes):
        deps = []
        for key in reads:
            ent = self.res.get(key[0], {})
            for k2, (w, rs) in ent.items():
                if self._overlap(key, k2):
                    if w is not None:
                        deps.append(w)
                    if key[0] == "ps":
                        deps.extend(r for r in rs if r["eng"] != eng)
        for key in writes:
            ent = self.res.get(key[0], {})
            for k2, (w, rs) in ent.items():
                if self._overlap(key, k2):
                    if w is not None:
                        deps.append(w)
                    deps.extend(rs)
        return deps

    def _record(self, op, reads, writes):
        for key in reads:
            ent = self.res.setdefault(key[0], {})
            if key not in ent:
                ent[key] = [None, []]
            ent[key][1].append(op)
        for key in writes:
            ent = self.res.setdefault(key[0], {})
            for k2 in [k for k in ent if self._overlap(key, k) and k != key and len(k) > len(key)]:
                del ent[k2]
            ent[key] = [op, []]

    def _add(self, eng, fn, kind, reads, writes, dur, is_output=False):
        reads = self._norm(reads)
        writes = self._norm(writes)
        deps = self._deps_for(eng, reads, writes)
        op = {"id": self.nid, "eng": eng, "fn": fn, "kind": kind, "deps": {d["id"]: d for d in deps}, "dur": dur, "out": is_output}
        self.nid += 1
        self.seg.append(op)
        self._record(op, reads, writes)
        self.ninstr += 1
        return op

    def op(self, eng, fn, reads=None, writes=None, dur=0.3, tag=None):
        o = self._add(eng, fn, "eng", reads, writes, dur)
        o["tag"] = tag
        return o

    def dma(self, eng, fn, reads=None, writes=None, is_output=False, nbytes=1 << 20):
        return self._add(eng, fn, "dma", reads, writes, nbytes, is_output)

    def open(self, stack):
        nc = self.nc
        self.sems = {}
        for e in ENGS:
            self.sems[e] = stack.enter_context(nc.semaphore("p_" + e))
        for i in range(self.n_dma_sems):
            self.sems[("dma", i)] = stack.enter_context(nc.semaphore("d_%d" % i))

    def _order(self, ops):
        if not self.schedule:
            return {e: [o for o in ops if o["eng"] == e] for e in ENGS}
        import heapq
        ids = {o["id"] for o in ops}
        succ = {o["id"]: [] for o in ops}
        indeg = {}
        for o in ops:
            d = [k for k in o["deps"] if k in ids]
            indeg[o["id"]] = len(d)
            for k in d:
                succ[k].append(o)
        fin = {}
        cur_tab = [None]
        free = {e: 0.0 for e in ENGS}
        order = {e: [] for e in ENGS}
        ready_t = {o["id"]: 0.0 for o in ops}
        heap = [(0.0, o["id"], o) for o in ops if indeg[o["id"]] == 0]
        heapq.heapify(heap)
        while heap:
            rt, oid, o = heapq.heappop(heap)
            e = o["eng"]
            est = max(rt, free[e])
            tg = o.get("tag")
            if tg is not None and tg != cur_tab[0] and not o.get("_tq"):
                o["_tq"] = True
                heapq.heappush(heap, (est + 1.3, oid, o))
                continue
            if est > rt + 1e-9:
                if not o.get("_rq"):
                    o["_rq"] = True
                    heapq.heappush(heap, (est, oid, o))
                    continue
            start = est
            if tg is not None and tg != cur_tab[0]:
                cur_tab[0] = tg
                start += 1.3
            if o["kind"] == "dma":
                free[e] = start + 0.1
                done = start + self.DMA_FIX + o["dur"] / self.DMA_BW
            else:
                free[e] = start + o["dur"]
                done = free[e]
            fin[oid] = done
            order[e].append(o)
            for s in succ[oid]:
                lat = 0.1 if s["eng"] == e and o["kind"] != "dma" else self.LAT
                ready_t[s["id"]] = max(ready_t[s["id"]], done + lat)
                indeg[s["id"]] -= 1
                if indeg[s["id"]] == 0:
                    heapq.heappush(heap, (ready_t[s["id"]], s["id"], s))
        assert sum(len(v) for v in order.values()) == len(ops)
        return order

    def _waits(self, eng, sigs):
        best = {}
        for (sk, v) in sigs:
            if sk == "pe" and eng == "pe":
                continue
            if v > best.get(sk, 0):
                best[sk] = v
        waits = []
        for sk, v in best.items():
            if self.known[eng].get(sk, 0) >= v:
                continue
            self.known[eng][sk] = v
            waits.append((sk, v))
        return waits

    def barrier(self):
        self._emit_segment()
        toks = [(e, self.count[e]) for e in ENGS if e != "sp" and self.count[e] > 0]
        toks += [(("dma", i), v) for i, v in enumerate(self.dma_val) if v > 0]
        for e in ENGS:
            w = self._waits(e, [t for t in toks if t[0] != e])
            if w:
                self.pre[e].extend(w)
        self.res = {}

    def flush(self, final=False):
        if final:
            self._emit_segment(final=True)

    def _emit_segment(self, final=False):
        ops = self.seg
        self.seg = []
        order = self._order(ops)
        half = self.n_dma_sems // 2
        for e in ENGS:
            for o in order[e]:
                if o["kind"] == "eng":
                    self.count[e] += 1
                    o["sig"] = (e, self.count[e])
                else:
                    if e == "pool":
                        i = half + self.dma_rr_pool
                        self.dma_rr_pool = (self.dma_rr_pool + 1) % (self.n_dma_sems - half)
                    else:
                        i = self.dma_rr
                        self.dma_rr = (self.dma_rr + 1) % half
                    o["reuse"] = self.dma_last[i]
                    self.dma_val[i] += 16
                    o["sig"] = (("dma", i), self.dma_val[i])
                    self.dma_last[i] = o["sig"]
                    if o["out"]:
                        self.out_sigs.append(o["sig"])
        seg_ids = {o["id"] for o in ops}
        prog = {}
        for e in ENGS:
            lst = []
            pre = self.pre[e]
            self.pre[e] = []
            for o in order[e]:
                sigs = [d["sig"] for k, d in o["deps"].items() if k in seg_ids]
                if o["kind"] == "dma" and o["reuse"] is not None:
                    sigs.append(o["reuse"])
                lst.append((self._waits(e, sigs), o))
            prog[e] = (pre, lst)
        fin = self._waits("sp", self.out_sigs) if final else None
        nc = self.nc
        sems = self.sems

        def run(ename, h, fin_=None):
            pre, lst = prog[ename]
            for sk, v in pre:
                h.wait_ge(sems[sk], v)
            for waits, o in lst:
                for sk, v in waits:
                    h.wait_ge(sems[sk], v)
                ins = o["fn"](h)
                ins.then_inc(sems[o["sig"][0]], 16 if o["kind"] == "dma" else 1)
            if fin_:
                for sk, v in fin_:
                    h.wait_ge(sems[sk], v)

        def has(e):
            return bool(prog[e][0] or prog[e][1])

        if not any(has(e) for e in ENGS) and not fin:
            return
        with nc.Block() as block:
            if has("pe"):
                @block.tensor
                def _(h):
                    run("pe", h)
            if has("act"):
                @block.scalar
                def _(h):
                    run("act", h)
            if has("dve"):
                @block.vector
                def _(h):
                    run("dve", h)
            if has("pool"):
                @block.gpsimd
                def _(h):
                    run("pool", h)
            if has("sp") or fin:
                @block.sync
                def _(h):
                    run("sp", h, fin)


def _ap(ap, pat):
    return bass.AP(ap.tensor, ap.offset, pat)


def bc_mid(ap, n):
    a = [list(x) for x in ap.ap]
    return _ap(ap, [a[0], [0, n]] + a[1:])


def bc_last(ap, n):
    a = [list(x) for x in ap.ap]
    return _ap(ap, a + [[0, n]])


class _Stop(Exception):
    pass


def build(cfg, debug=(), limit=None):
    c = cfg
    nc = bass.Bass("TRN2", target_bir_lowering=False)
    DC, QC, IC, XC, SH, SG, FC, NKV, IH = c.DC, c.QC, c.IC, c.XC, c.SH, c.SG, c.FC, c.NKV, c.IH
    NSEQ, TT, T, NS, DS, PAST, SEQ = c.NSEQ, c.TT, c.T, c.NS, c.DS, c.PAST, c.SEQ
    NXB = c.NXB
    streams = c.streams()

    def din(name, shape):
        return nc.dram_tensor(name, list(shape), F32, kind="ExternalInput").ap()

    def dout(name, shape):
        return nc.dram_tensor(name, list(shape), F32, kind="ExternalOutput").ap()

    xT_d = din("xT", [128, DC, TT])
    cT_d = din("cT", [128, DC, NSEQ])
    rope_d = din("rope", [128, 2, TT])
    cf_d = din("cf32", [128, c.NF])
    cb_d = din("cbf", [128, c.NB])
    vec_d = din("vecs", [128, c.NV])
    w_d = {}
    for sname, (KC, pieces) in streams.items():
        mx = max(len(p) for p in pieces) * 128 * KC
        w_d[sname] = din("w_" + sname, [len(pieces), 128, mx])
    wtok_d = din("w_tok", [128, DC * c.NTOK])
    ckT_d = din("ckT", [NS, 64, NKV, PAST])
    ckiT_d = din("ckiT", [NS, 64, PAST])
    cv_d = din("cv", [NS, 128, PAST // 128, NKV * 64])
    sconv_d = din("sconv", [128, NXB, NS, 3])
    sssm_d = din("sssm", [NS, SG, 128, 256])
    o_y = dout("o_y", [128, DC, TT])
    o_k = dout("o_k", [64, NKV, TT])
    o_ki = dout("o_ki", [64, TT])
    o_v = dout("o_v", [TT, NKV * 64])
    o_conv = dout("o_conv", [128, NXB, NSEQ, 3])
    o_ssm = dout("o_ssm", [NSEQ, SG, 128, 256])
    dbg_outs = {}

    with contextlib.ExitStack() as st:
        P = Prog(nc)
        P.open(st)

        uid = {"n": 0}

        def sb(name, shape, dt, stack=st):
            uid["n"] += 1
            return stack.enter_context(nc.sbuf_tensor("%s_%d" % (name, uid["n"]), list(shape), dt))

        def DM(eng, out, in_, reads=None, writes=None, is_output=False):
            return P.dma(eng, lambda h: h.dma_start(out=out, in_=in_), reads=reads, writes=writes, is_output=is_output)

        ps = [st.enter_context(nc.psum_tensor("ps%d" % i, [128, 512], F32)) for i in range(8)]
        psb = ps[7][:, :].bitcast(BF16)

        cf = sb("cf", [128, c.NF], F32)
        cb = sb("cb", [128, c.NB], BF16)
        vec = sb("vec", [128, c.NV], F32)
        NWB = 4
        wbuf = [sb("wbuf%d" % i, [128, WBUF_ELEMS], BF16) for i in range(NWB)]
        wtok = sb("wtok", [128, DC, c.NTOK], BF16)
        modT = sb("modT", [128, 6 * DC, NSEQ], F32)
        G1 = sb("G1", [128, DC, NSEQ], F32)
        G2 = sb("G2", [128, DC, NSEQ], F32)
        a_b = sb("a_b", [128, SH], F32)
        xT = sb("xT", [128, DC, T], F32)
        hT = sb("hT", [128, DC, T], BF16)
        qT = sb("qT", [128, QC, T], BF16)
        y3T = sb("y3T", [128, XC, T], BF16)
        kT = sb("kT", [128, NKV, c.LMAX], BF16)
        kiT = sb("kiT", [128, c.LMAX], BF16)
        Vc = sb("Vc", [128, c.NBMAX, NKV * 64], BF16)
        hst = sb("hst", [128, SG, 256], F32)
        chist = sb("chist", [128, NXB, 3], BF16)
        cst = sb("cst", [128, NXB, NSEQ, 3], F32)
        NBLK = T // 128 if T // 128 > NS else NS
        wis = sb("wis", [128, NBLK, IH], F32)
        dtp = sb("dtp", [128, NBLK, SH], F32)
        dAt = sb("dAt", [128, NBLK, SH], F32)

        def cfv(name, a=0, b=None):
            o, w = c.cf_off[name]
            return cf[:, o + a: o + (w if b is None else b)]

        def cbv(name):
            o, w = c.cb_off[name]
            return cb[:, o:o + w]

        def vv(name, a=0, b=None):
            o, w = c.vec_off[name]
            return vec[:, o + a: o + (w if b is None else b)]

        def dbg(name, ap, shape, key):
            if name not in debug:
                return
            d = nc.dram_tensor("dbg_" + name, list(shape), ap.dtype, kind="ExternalOutput").ap()
            dbg_outs[name] = d
            P.dma("sp", lambda h: h.dma_start(out=d, in_=ap), reads=[key], is_output=True)

        plan = []
        plan += [("ada", i) for i in range(len(streams["ada"][1]))]

        def tile_plan():
            pl = []
            for s in ("b1", "b2", "gate", "ba", "bs", "out", "gu", "dn"):
                pl += [(s, i) for i in range(len(streams[s][1]))]
            return pl
        import os as _os
        _nt = len(_os.environ["TILES"].split(",")) if "TILES" in _os.environ else c.NT + 1
        for _ in range(_nt):
            plan += tile_plan()
        wq = {"next_issue": 0, "next_use": 0}

        def w_issue(k):
            sname, pi = plan[k]
            KC, pieces = streams[sname]
            n = KC * len(pieces[pi]) * 128
            slot = k % NWB
            src = w_d[sname][pi, :, 0:n]
            dst = wbuf[slot][:, 0:n]
            P.dma("pool", lambda h: h.dma_start(out=dst, in_=src), writes=[("wbuf", slot)])

        def w_get(sname, pi):
            k = wq["next_use"]
            assert plan[k] == (sname, pi), (plan[k], sname, pi)
            while wq["next_issue"] < min(len(plan), k + NWB):
                w_issue(wq["next_issue"])
                wq["next_issue"] += 1
            wq["next_use"] += 1
            KC, pieces = streams[sname]
            ncols = len(pieces[pi]) * 128
            slot = k % NWB
            view = wbuf[slot][:, 0:KC * ncols].rearrange("p (k n) -> p k n", k=KC)
            return view, ("wbuf", slot)

        dense_rr = {"i": 0}
        phase = {"n": 0}

        def phase_done():
            phase["n"] += 1
            return limit is not None and phase["n"] >= limit

        def dense_fm(sname, KC, rhs_fn, TW, handler, rd_keys):
            _, pieces = streams[sname]
            for pi, chunks in enumerate(pieces):
                wv, wkey = w_get(sname, pi)
                for j, cid in enumerate(chunks):
                    bank = dense_rr["i"] % 3
                    dense_rr["i"] += 1
                    for kc in range(KC):
                        lhsT = wv[:, kc, j * 128:(j + 1) * 128]
                        rhs = rhs_fn(kc)
                        out = ps[bank][:, 0:TW]
                        mm(out, lhsT, rhs, kc == 0, kc == KC - 1, [wkey] + rd_keys, [("ps", bank)])
                    handler(cid, ps[bank][:, 0:TW], ("ps", bank))

        def est(eng, meth, a, kw):
            try:
                if eng == "pe":
                    rhs = a[2] if meth == "matmul" else a[1]
                    n = rhs.free_size()
                    if meth == "matmul":
                        return 0.03 + max(64, n) * (4 if rhs.dtype == F32 else 1) / 2400.0
                    return 0.12
                ap = kw.get("out", a[0] if a else None)
                n = ap.free_size()
                if eng == "act":
                    return 0.22 + n / 1200.0
                if eng == "dve":
                    return 0.10 + n / 960.0
                return 0.20 + 2.0 * n / 1200.0
            except Exception:
                return 0.3

        TABS = {AF.Silu: "silu", AF.Sigmoid: "sigmoid", AF.Exp: "exp", AF.Ln: "exp"}

        def E(eng, meth, reads, writes, *a, **kw):
            tag = TABS.get(kw.get("func")) if eng == "act" else None
            return P.op(eng, lambda h: getattr(h, meth)(*a, **kw), reads=reads, writes=writes, dur=est(eng, meth, a, kw), tag=tag)

        def act(out, in_, func, reads, writes, **kw):
            return E("act", "activation", reads, writes, out=out, in_=in_, func=func, **kw)

        def tt(eng, out, in0, in1, op, reads, writes):
            return E(eng, "tensor_tensor", reads, writes, out=out, in0=in0, in1=in1, op=op)

        def ts(eng, out, in0, s1, s2, op0, op1, reads, writes, **kw):
            if op1 is None:
                return E(eng, "tensor_scalar", reads, writes, out=out, in0=in0, scalar1=s1, scalar2=None, op0=op0, **kw)
            return E(eng, "tensor_scalar", reads, writes, out=out, in0=in0, scalar1=s1, scalar2=s2, op0=op0, op1=op1, **kw)

        def stt(eng, out, in0, scalar, in1, op0, op1, reads, writes):
            return E(eng, "scalar_tensor_tensor", reads, writes, out=out, in0=in0, scalar=scalar, in1=in1, op0=op0, op1=op1)

        def mm(out, lhsT, rhs, start, stop, reads, writes, **kw):
            return E("pe", "matmul", reads, writes, out, lhsT, rhs, start=start, stop=stop, **kw)

        def mmg(items, reads, writes):
            def fn(h):
                ins = None
                for (o, l, r, s0, s1, kw) in items:
                    ins = h.matmul(o, l, r, start=s0, stop=s1, **kw)
                return ins
            d = sum(est("pe", "matmul", (o, l, r), {}) for (o, l, r, s0, s1, kw) in items) * 0.6
            return P.op("pe", fn, reads=reads, writes=writes, dur=d)

        P.dma("sp", lambda h: h.dma_start(out=cf[:], in_=cf_d), writes=["cf"])
        P.dma("pool", lambda h: h.dma_start(out=cb[:], in_=cb_d), writes=["cb"])
        P.dma("sp", lambda h: h.dma_start(out=vec[:], in_=vec_d), writes=["vec"])
        P.dma("pool", lambda h: h.dma_start(out=wtok[:].rearrange("p k n -> p (k n)"), in_=wtok_d), writes=["wtok"])
        epsc = sb("epsc", [128, 1], F32)
        E("dve", "memset", [], ["epsc"], epsc[:], EPS)
        E("dve", "memset", [], ["hst"], hst[:], 0.0)
        E("dve", "memset", [], ["chist"], chist[:], 0.0)
        E("dve", "memset", [], ["cst"], cst[:], 0.0)
        with contextlib.ExitStack() as s0:
            cTs = sb("cTs", [128, DC, NSEQ], F32, s0)
            scT = sb("scT", [128, DC, NSEQ], BF16, s0)
            P.dma("sp", lambda h: h.dma_start(out=cTs[:], in_=cT_d), writes=["cTs"])
            act(scT[:], cTs[:], AF.Silu, ["cTs"], ["scT"])
            act(a_b[:], vv("alog"), AF.Exp, ["vec"], ["a_b"])
            ts("dve", a_b[:], a_b[:], -1.0, None, ALU.mult, None, ["a_b"], ["a_b"])
            _, pieces = streams["ada"]
            for pi, chunks in enumerate(pieces):
                wv, wkey = w_get("ada", pi)
                for j, cid in enumerate(chunks):
                    for kc in range(DC):
                        mm(ps[3][:, cid * NSEQ:(cid + 1) * NSEQ], wv[:, kc, j * 128:(j + 1) * 128], scT[:, kc, :],
                           kc == 0, kc == DC - 1, [wkey, "scT"], [("ps", 3)], skip_group_check=True)
            tt("dve", modT[:], ps[3][:, 0:6 * DC * NSEQ].rearrange("p (a b) -> p a b", b=NSEQ),
               bc_last(vv("bada"), NSEQ), ALU.add, [("ps", 3), "vec"], ["modT"])
            stt("dve", G1[:], modT[:, 1 * DC:2 * DC, :], 1.0, bc_last(vv("gmix"), NSEQ), ALU.add, ALU.mult, ["modT", "vec"], ["G1"])
            stt("dve", G2[:], modT[:, 4 * DC:5 * DC, :], 1.0, bc_last(vv("gffn"), NSEQ), ALU.add, ALU.mult, ["modT", "vec"], ["G2"])
            dbg("modT", modT[:], [128, 6 * DC, NSEQ], "modT")
            P.barrier()
            P.flush()
        phase_done_flag = True

        SH1 = lambda ch, s: modT[:, 0 * DC + ch, s:s + 1]
        GT1 = lambda ch, s: modT[:, 2 * DC + ch, s:s + 1]
        SH2 = lambda ch, s: modT[:, 3 * DC + ch, s:s + 1]
        GT2 = lambda ch, s: modT[:, 5 * DC + ch, s:s + 1]

        def rms_mod(src, G, SHf, dst, TW, segs, sc):
            sqb = [sb("rm_sq%d" % i, [128, T], BF16, sc) for i in range(2)]
            rs1 = sb("rm_rs1", [128, T], F32, sc)
            rstd = sb("rm_rstd", [128, T], F32, sc)
            tmp = [sb("rm_tmp%d" % i, [128, T], F32, sc) for i in range(2)]
            for ch in range(DC):
                act(sqb[ch % 2][:, 0:TW], src[:, ch, 0:TW], AF.Square, ["xT"], [("rm_sq", ch % 2)])
                mm(ps[3][:, 0:TW], cbv("ones"), sqb[ch % 2][:, 0:TW], ch == 0, ch == DC - 1, [("rm_sq", ch % 2), "cb"], [("ps", 3)])
            act(rs1[:, 0:TW], ps[3][:, 0:TW], AF.Ln, [("ps", 3)], ["rm_rs1"], scale=1.0 / c.D, bias=epsc[:, 0:1])
            act(rstd[:, 0:TW], rs1[:, 0:TW], AF.Exp, ["rm_rs1"], ["rm_rstd"], scale=-0.5)
            for ch in range(DC):
                tt("dve", tmp[ch % 2][:, 0:TW], src[:, ch, 0:TW], rstd[:, 0:TW], ALU.mult, ["xT", "rm_rstd"], [("rm_tmp", ch % 2)])
                for (s, t0, ln) in segs:
                    act(dst[:, ch, t0:t0 + ln], tmp[ch % 2][:, t0:t0 + ln], AF.Identity, [("rm_tmp", ch % 2), "modT", "G1", "G2"],
                        [("hT", ch)], scale=G[:, ch, s:s + 1], bias=SHf(ch, s))

        def do_tile(ti):
            is_p = ti < c.NT
            if is_p:
                TW = T
                tok0 = ti * T
                segs = [(0, 0, T)]
                SL = T
            else:
                TW = NS * DS
                tok0 = SEQ
                segs = [(1 + s, s * DS, DS) for s in range(NS)]
                SL = DS
            NSEG = len(segs)
            BL = 128 if is_p else DS
            NB = TW // BL

            with contextlib.ExitStack() as s12:
                knew = sb("knew", [128, NKV, T], BF16, s12)
                qiT = sb("qiT", [128, IC, T], BF16, s12)
                kinew = sb("kinew", [128, T], BF16, s12)
                with contextlib.ExitStack() as s1:
                    ropet = sb("ropet", [128, 2, T], F32, s1)
                    P.dma("sp", lambda h: h.dma_start(out=xT[:, :, 0:TW], in_=xT_d[:, :, tok0:tok0 + TW]), writes=["xT"])
                    P.dma("sp", lambda h: h.dma_start(out=ropet[:, :, 0:TW], in_=rope_d[:, :, tok0:tok0 + TW]), writes=["ropet"])
                    rms_mod(xT, G1, SH1, hT, TW, segs, s1)
                    dbg("hT%d" % ti, hT[:, :, 0:TW], [128, DC, TW], "hT")
                    RD = 3
                    sqb = [sb("b1_sq%d" % i, [128, T], BF16, s1) for i in range(RD)]
                    tA = [sb("b1_tA%d" % i, [128, T], F32, s1) for i in range(RD)]
                    tB = tA
                    tC = [sb("b1_tC%d" % i, [128, T], F32, s1) for i in range(RD)]
                    tD = [sb("b1_tD%d" % i, [128, T], F32, s1) for i in range(RD)]
                    qn = [sb("b1_qn%d" % i, [128, T], BF16, s1) for i in range(RD)]
                    kst = [sb("b1_kst%d" % i, [128, T], F32, s1) for i in range(RD)]
                    vst = [sb("b1_vst%d" % i, [128, c.NTOK], F32, s1) for i in range(2)]
                    sp1 = sb("b1_sp1", [128, SH], F32, s1)
                    sp2 = sb("b1_sp2", [128, SH], F32, s1)
                    rr = {"i": 0}

                    def rope(qn_ap, dst_ap, i, dst_key, extra_reads):
                        rb = 3 + i % 3
                        mm(ps[rb][:, 0:TW], cbv("rot"), qn_ap, True, True, extra_reads + ["cb"], [("ps", rb)])
                        tt("pool", tC[i % RD][:, 0:TW], qn_ap, ropet[:, 0, 0:TW], ALU.mult, extra_reads + ["ropet"], [("b1_tC", i % RD)])
                        tt("dve", tD[i % RD][:, 0:TW], ps[rb][:, 0:TW], ropet[:, 1, 0:TW], ALU.mult, [("ps", rb), "ropet"], [("b1_tD", i % RD)])
                        tt("dve", dst_ap, tC[i % RD][:, 0:TW], tD[i % RD][:, 0:TW], ALU.add, [("b1_tC", i % RD), ("b1_tD", i % RD)], [dst_key])

                    def b1_handler(cid, pp, pkey):
                        i = rr["i"]
                        rr["i"] += 1
                        b = i % RD
                        sqk = 6 + i % 2
                        if cid < QC + NKV:
                            gcol = vv("gq") if cid < QC else vv("gk")
                            act(sqb[b][:, 0:TW], pp, AF.Square, [pkey], [("b1_sq", b)])
                            mm(ps[sqk][:, 0:TW], cbv("blk"), sqb[b][:, 0:TW], True, True, [("b1_sq", b), "cb"], [("ps", sqk)])
                            act(tA[b][:, 0:TW], ps[sqk][:, 0:TW], AF.Ln, [("ps", sqk)], [("b1_tA", b)], scale=1.0 / 64, bias=epsc[:, 0:1])
                            act(tB[b][:, 0:TW], tA[b][:, 0:TW], AF.Exp, [("b1_tA", b)], [("b1_tA", b)], scale=-0.5)
                            stt("dve", qn[b][:, 0:TW], pp, gcol, tB[b][:, 0:TW], ALU.mult, ALU.mult, [pkey, "vec", ("b1_tA", b)], [("b1_qn", b)])
                        else:
                            act(qn[b][:, 0:TW], pp, AF.Copy, [pkey], [("b1_qn", b)])
                        if cid < QC:
                            rope(qn[b][:, 0:TW], qT[:, cid, 0:TW], i, ("qT", cid), [("b1_qn", b)])
                        elif cid < QC + NKV:
                            g = cid - QC
                            rope(qn[b][:, 0:TW], kst[b][:, 0:TW], i, ("b1_kst", b), [("b1_qn", b)])
                            act(knew[:, g, 0:TW], kst[b][:, 0:TW], AF.Copy, [("b1_kst", b)], [("knew", g)])
                            DM("sp", o_k[:, g, tok0:tok0 + TW], kst[b][0:64, 0:TW], reads=[("b1_kst", b)], is_output=True)
                        elif cid < QC + NKV + IC:
                            rope(qn[b][:, 0:TW], qiT[:, cid - QC - NKV, 0:TW], i, ("qiT", cid - QC - NKV), [("b1_qn", b)])
                        else:
                            rope(qn[b][:, 0:TW], kst[b][:, 0:TW], i, ("b1_kst", b), [("b1_qn", b)])
                            act(kinew[:, 0:TW], kst[b][:, 0:TW], AF.Copy, [("b1_kst", b)], ["kinew"])
                            DM("sp", o_ki[:, tok0:tok0 + TW], kst[b][0:64, 0:TW], reads=[("b1_kst", b)], is_output=True)

                    import os as _os
                    SUB = int(_os.environ.get("SUB", "9")) if not is_p else 9
                    if SUB >= 2:
                        dense_fm("b1", DC, lambda kc: hT[:, kc, 0:TW], TW, b1_handler, ["hT"])
                    for blk in range(NB):
                        t0 = blk * BL
                        bank = 4 + blk % 2
                        for kc in range(DC):
                            mm(ps[bank][0:BL, 0:c.NTOK], hT[:, kc, t0:t0 + BL], wtok[:, kc, :], kc == 0, kc == DC - 1,
                               ["hT", "wtok"], [("ps", bank)])
                        b = blk % 2
                        act(vst[b][0:BL, :], ps[bank][0:BL, 0:c.NTOK], AF.Copy, [("ps", bank)], [("b1_vst", b)])
                        pv = vst[b]
                        DM("sp", o_v[tok0 + t0:tok0 + t0 + BL, :], vst[b][0:BL, 0:NKV * 64], reads=[("b1_vst", b)], is_output=True)
                        kb = (tok0 + t0) // 128 if is_p else PAST // 128
                        if is_p:
                            E("pool", "tensor_copy", [("b1_vst", b)], [("Vc", kb)], out=Vc[0:BL, kb, :], in_=pv[0:BL, 0:NKV * 64])
                        else:
                            E("pool", "tensor_copy", [("b1_vst", b)], [("vnew", blk)], out=vnew[0:BL, blk, :], in_=pv[0:BL, 0:NKV * 64])
                        ts("dve", wis[0:BL, blk, :], pv[0:BL, NKV * 64:NKV * 64 + IH], float(IH ** -0.5 * 64 ** -0.5), None, ALU.mult, None,
                           [("b1_vst", b)], [("wis", blk)])
                        tt("dve", sp1[0:BL, :], pv[0:BL, NKV * 64 + IH:c.NTOK], vv("dtb")[0:BL, :], ALU.add, [("b1_vst", b), "vec"], ["b1_sp1"])
                        act(sp2[0:BL, :], sp1[0:BL, :], AF.Abs, ["b1_sp1"], ["b1_sp2"])
                        act(sp2[0:BL, :], sp2[0:BL, :], AF.Exp, ["b1_sp2"], ["b1_sp2"], scale=-1.0)
                        act(sp2[0:BL, :], sp2[0:BL, :], AF.Ln, ["b1_sp2"], ["b1_sp2"], bias=1.0)
                        stt("dve", dtp[0:BL, blk, :], sp1[0:BL, :], 0.0, sp2[0:BL, :], ALU.max, ALU.add, ["b1_sp1", "b1_sp2"], [("dtp", blk)])
                        tt("dve", dAt[0:BL, blk, :], dtp[0:BL, blk, :], a_b[0:BL, :], ALU.mult, [("dtp", blk), "a_b"], [("dAt", blk)])
                    if is_p:
                        E("pool", "tensor_copy", ["knew"], [("kT", ti)], out=kT[:, :, tok0:tok0 + TW], in_=knew[:, :, 0:TW])
                        E("pool", "tensor_copy", ["kinew"], [("kiT", ti)], out=kiT[:, tok0:tok0 + TW], in_=kinew[:, 0:TW])
                    dbg("qT%d" % ti, qT[:, :, 0:TW], [128, QC, TW], "qT")
                    dbg("knew%d" % ti, knew[:, :, 0:TW], [128, NKV, TW], "knew")
                    dbg("qiT%d" % ti, qiT[:, :, 0:TW], [128, IC, TW], "qiT")
                    dbg("dtp%d" % ti, dtp[:], [128, NBLK, SH], "dtp")
                    dbg("wis%d" % ti, wis[:], [128, NBLK, IH], "wis")
                    P.barrier()
                    P.flush()
                stop = phase_done()

                with contextlib.ExitStack() as s2:
                  if not stop:
                    attention(ti, is_p, TW, tok0, knew, qiT, kinew, s2)
                    if ti == 0:
                        print("SBUF remaining in attention scope:", nc.sbuf_bytes_remaining)
                    dbg("oT%d" % ti, qT[:, :, 0:TW], [128, QC, TW], "qT")
                    P.barrier()
                    P.flush()
                    stop = phase_done()
            if stop:
                return True

            with contextlib.ExitStack() as s3:
                ssd_phase(ti, is_p, TW, tok0, segs, SL, s3)
                if ti == 0:
                    print("SBUF remaining in ssd scope:", nc.sbuf_bytes_remaining)
                dbg("y3T%d" % ti, y3T[:, :, 0:TW], [128, XC, TW], "y3T")
                P.barrier()
                P.flush()
            if phase_done():
                return True

            with contextlib.ExitStack() as s4:
                gsig = sb("gsig", [128, 2 * DC, T], BF16, s4)
                mixT = sb("mixT", [128, DC, T], BF16, s4)

                def gate_handler(cid, pp, pkey):
                    act(gsig[:, cid, 0:TW], pp, AF.Sigmoid, [pkey], [("gsig", cid)])
                dense_fm("gate", DC, lambda kc: hT[:, kc, 0:TW], TW, gate_handler, ["hT"])

                def ba_handler(cid, pp, pkey):
                    tt("dve", mixT[:, cid, 0:TW], pp, gsig[:, cid, 0:TW], ALU.mult, [pkey, ("gsig", cid)], [("mixT", cid)])
                dense_fm("ba", QC, lambda kc: qT[:, kc, 0:TW], TW, ba_handler, ["qT"])
                m2 = [sb("e_m2%d" % i, [128, T], F32, s4) for i in range(2)]
                rr2 = {"i": 0}

                def bs_handler(cid, pp, pkey):
                    b = rr2["i"] % 2
                    rr2["i"] += 1
                    tt("dve", m2[b][:, 0:TW], pp, gsig[:, DC + cid, 0:TW], ALU.mult, [pkey, ("gsig", DC + cid)], [("e_m2", b)])
                    tt("pool", mixT[:, cid, 0:TW], m2[b][:, 0:TW], mixT[:, cid, 0:TW], ALU.add, [("e_m2", b), ("mixT", cid)], [("mixT", cid)])
                dense_fm("bs", XC, lambda kc: y3T[:, kc, 0:TW], TW, bs_handler, ["y3T"])
                dbg("mixT%d" % ti, mixT[:, :, 0:TW], [128, DC, TW], "mixT")

                def out_handler(cid, pp, pkey):
                    for (s, t0, ln) in segs:
                        stt("dve", xT[:, cid, t0:t0 + ln], pp[:, t0:t0 + ln], GT1(cid, s), xT[:, cid, t0:t0 + ln], ALU.mult, ALU.add,
                            [pkey, "modT", ("xT", cid)], [("xT", cid)])
                dense_fm("out", DC, lambda kc: mixT[:, kc, 0:TW], TW, out_handler, ["mixT"])
                dbg("x1T%d" % ti, xT[:, :, 0:TW], [128, DC, TW], "xT")
                P.barrier()
                P.flush()
            if phase_done():
                return True

            with contextlib.ExitStack() as s5:
                rms_mod(xT, G2, SH2, hT, TW, segs, s5)
                aT = sb("aT", [128, FC, T], BF16, s5)
                sgb = [sb("f_sg%d" % i, [128, T], BF16, s5) for i in range(3)]
                yst = [sb("f_y%d" % i, [128, T], F32, s5) for i in range(2)]
                npair = max(len(p) for p in streams["gu"][1]) // 2

                def gu_handler(cid, pp, pkey):
                    j, isup = cid // 2, cid % 2
                    b = j % 3
                    if not isup:
                        act(sgb[b][:, 0:TW], pp, AF.Silu, [pkey], [("f_sg", b)])
                    else:
                        tt("dve", aT[:, j, 0:TW], pp, sgb[b][:, 0:TW], ALU.mult, [pkey, ("f_sg", b)], [("aT", j)])
                dense_fm("gu", DC, lambda kc: hT[:, kc, 0:TW], TW, gu_handler, ["hT"])
                rr3 = {"i": 0}

                def dn_handler(cid, pp, pkey):
                    b = rr3["i"] % 2
                    rr3["i"] += 1
                    for (s, t0, ln) in segs:
                        stt("dve", yst[b][:, t0:t0 + ln], pp[:, t0:t0 + ln], GT2(cid, s), xT[:, cid, t0:t0 + ln], ALU.mult, ALU.add,
                            [pkey, "modT", ("xT", cid)], [("f_y", b)])
                    P.dma("sp", lambda h: h.dma_start(out=o_y[:, cid, tok0:tok0 + TW], in_=yst[b][:, 0:TW]), reads=[("f_y", b)], is_output=True)
                dense_fm("dn", FC, lambda kc: aT[:, kc, 0:TW], TW, dn_handler, ["aT"])
                P.barrier()
                P.flush()
            return phase_done()

        vnew = sb("vnew", [128, NS, NKV * 64], BF16)

        def attention(ti, is_p, TW, tok0, knew, qiT, kinew, sc):
            LM = c.LMAX
            scores = [sb("at_score%d" % i, [128, LM], F32, sc) for i in range(2)]
            rts = [[sb("at_rt%d_%d" % (i, j), [128, 512], F32, sc) for j in range(2)] for i in range(2)]
            maskqs = [sb("at_maskq%d" % i, [128, LM], BF16, sc) for i in range(2)]
            maskTs = [sb("at_maskT%d" % i, [128, c.NBMAX, 128], BF16, sc) for i in range(2)]
            ep = [sb("at_ep%d" % i, [128, 512], BF16, sc) for i in range(2)]
            pp_ = [sb("at_pp%d" % i, [128, 512], BF16, sc) for i in range(2)]
            sms = [sb("at_sm%d" % i, [128, 64], F32, sc) for i in range(2)]
            steps2s = [sb("at_steps%d" % i, [128, 32], F32, sc) for i in range(2)]
            steps2ns = [sb("at_stepsn%d" % i, [128, 32], F32, sc) for i in range(2)]
            rden = [sb("at_rden%d" % i, [128, 256], F32, sc) for i in range(2)]
            NIT = c.NIT
            if is_p:
                qblocks = [(None, qb * 128, 128) for qb in range(TW // 128)]
            else:
                qblocks = [(s, s * DS, DS) for s in range(NS)]

            def geom(qi_):
                sidx, q0, QB = qblocks[qi_]
                if is_p:
                    gb = (tok0 + q0) // 128
                    L = (gb + 1) * 128
                    K = c.KP
                else:
                    L = PAST + DS
                    K = c.KS
                nkb = (L + 127) // 128
                kws = [min(128, L - kb * 128) for kb in range(nkb)]
                return sidx, q0, QB, L, K, nkb, kws

            def mask_stage(qi_):
                sidx, q0, QB, L, K, nkb, kws = geom(qi_)
                maskT = maskTs[qi_ % 2]
                mk = "at_maskT%d" % (qi_ % 2)
                par = qi_ % 2
                score, maskq, sm, steps2, steps2n, rt = scores[par], maskqs[par], sms[par], steps2s[par], steps2ns[par], rts[par]
                blk = qi_
                if not is_p:
                    s = sidx
                    DM("pool", kiT[0:64, 0:PAST], ckiT_d[s], writes=["kiT"])
                    DM("pool", kiT[64:128, 0:PAST], ckiT_d[s], writes=["kiT"])
                    E("pool", "tensor_copy", ["kinew"], ["kiT"], out=kiT[:, PAST:PAST + DS], in_=kinew[:, q0:q0 + DS])
                nkc = (L + 511) // 512
                ri = 0
                for kc in range(nkc):
                    w = min(512, L - kc * 512)
                    for h2 in range(0, IH, 2):
                        mmg([(ps[(4, 7)[hh % 2]][0:QB, 0:w], qiT[(hh % 2) * 64:(hh % 2) * 64 + 64, hh // 2, q0:q0 + QB],
                              kiT[(hh % 2) * 64:(hh % 2) * 64 + 64, kc * 512:kc * 512 + w], True, True, {}) for hh in (h2, h2 + 1)],
                            ["qiT", "kiT"], [("ps", 4), ("ps", 7)])
                        for hh in (h2, h2 + 1):
                            bank = (4, 7)[hh % 2]
                            act(rt[ri % 2][0:QB, 0:w], ps[bank][0:QB, 0:w], AF.Relu, [("ps", bank)], [("at_rt%d" % par, ri % 2)])
                            if hh == 0:
                                ts("dve", score[0:QB, kc * 512:kc * 512 + w], rt[ri % 2][0:QB, 0:w], wis[0:QB, blk, hh:hh + 1], None, ALU.mult, None,
                                   [("at_rt%d" % par, ri % 2), "wis"], [("at_score%d" % par, kc)])
                            else:
                                stt("dve", score[0:QB, kc * 512:kc * 512 + w], rt[ri % 2][0:QB, 0:w], wis[0:QB, blk, hh:hh + 1],
                                    score[0:QB, kc * 512:kc * 512 + w], ALU.mult, ALU.add, [("at_rt%d" % par, ri % 2), "wis", ("at_score%d" % par, kc)],
                                    [("at_score%d" % par, kc)])
                            ri += 1
                        yield
                E("dve", "tensor_reduce", ["at_score%d" % par], [("at_sm%d" % par, 0)], out=sm[0:QB, 0:1], in_=score[0:QB, 0:L], axis=AX.X, op=ALU.max,
                  apply_absolute_value=True)
                ts("dve", sm[0:QB, 0:1], sm[0:QB, 0:1], 1.0001, 1e-20, ALU.mult, ALU.add, [("at_sm%d" % par, 0)], [("at_sm%d" % par, 0)])
                ts("dve", steps2[0:QB, 0:NIT + 1], cfv("pow2", 0, NIT + 1)[0:QB, :], sm[0:QB, 0:1], None, ALU.mult, None, ["cf", ("at_sm%d" % par, 0)], ["at_steps%d" % par])
                if is_p:
                    tt("dve", score[0:QB, L - 128:L], score[0:QB, L - 128:L], cfv("mb2"), ALU.add, ["at_score%d" % par, "cf"], ["at_score%d" % par])
                E("dve", "memset", [], [("at_sm%d" % par, 1)], sm[0:QB, 1:2], 0.0)
                ts("dve", steps2n[0:QB, 0:NIT + 1], steps2[0:QB, 0:NIT + 1], -1.0, None, ALU.mult, None, ["at_steps%d" % par], ["at_stepsn%d" % par])
                yield
                thrK = float(2 * K - L)
                for it in range(1, NIT + 1):
                    act(maskq[0:QB, 0:L], score[0:QB, 0:L], AF.Sign, ["at_score%d" % par, ("at_sm%d" % par, 1)], ["at_maskq%d" % par, ("at_sm%d" % par, 2)],
                        bias=sm[0:QB, 1:2], accum_out=sm[0:QB, 2:3])
                    yield
                    ts("dve", sm[0:QB, 3:4], sm[0:QB, 2:3], thrK - 0.5, 0.5, ALU.is_gt, ALU.subtract, [("at_sm%d" % par, 2)], [("at_sm%d" % par, 3)])
                    yield
                    stt("dve", sm[0:QB, 1:2], sm[0:QB, 3:4], steps2n[0:QB, it:it + 1], sm[0:QB, 1:2], ALU.mult, ALU.add,
                        [("at_sm%d" % par, 3), "at_stepsn%d" % par, ("at_sm%d" % par, 1)], [("at_sm%d" % par, 1)])
                    yield
                stt("dve", sm[0:QB, 4:5], steps2[0:QB, NIT:NIT + 1], 0.5, sm[0:QB, 1:2], ALU.mult, ALU.add, ["at_steps%d" % par, ("at_sm%d" % par, 1)], [("at_sm%d" % par, 4)])
                yield
                ts("dve", maskq[0:QB, 0:L], score[0:QB, 0:L], sm[0:QB, 4:5], 0.0, ALU.add, ALU.is_ge, ["at_score%d" % par, ("at_sm%d" % par, 4)], ["at_maskq%d" % par])
                yield
                for kb0 in range(0, nkb, 4):
                    nb_ = min(4, nkb - kb0)
                    for j in range(nb_):
                        kb = kb0 + j
                        E("pe", "transpose", ["at_maskq%d" % par, "cb"], [("ps", 7)], psb[0:kws[kb], j * 128: j * 128 + QB],
                          maskq[0:QB, kb * 128:kb * 128 + kws[kb]], cbv("ident")[0:QB, 0:QB])
                    for j in range(nb_):
                        kb = kb0 + j
                        act(maskT[0:kws[kb], kb, 0:QB], psb[0:kws[kb], j * 128: j * 128 + QB], AF.Copy,
                            [("ps", 7)], [(mk, kb)])
                    yield

            def attend_stage(qi_):
                sidx, q0, QB, L, K, nkb, kws = geom(qi_)
                maskT = maskTs[qi_ % 2]
                mk = "at_maskT%d" % (qi_ % 2)
                if not is_p:
                    s = sidx
                    for g_ in range(NKV):
                        DM("pool", kT[0:64, g_, 0:PAST], ckT_d[s][:, g_, :], writes=[("kT", g_)])
                        DM("pool", kT[64:128, g_, 0:PAST], ckT_d[s][:, g_, :], writes=[("kT", g_)])
                        E("pool", "tensor_copy", ["knew"], [("kT", g_)], out=kT[:, g_, PAST:PAST + DS], in_=knew[:, g_, q0:q0 + DS])
                    DM("pool", Vc[:, 0:PAST // 128, :], cv_d[s], writes=["Vc"])
                    E("pool", "tensor_copy", [("vnew", s)], ["Vc"], out=Vc[0:DS, PAST // 128, :], in_=vnew[0:DS, s, :])
                steps = [(g, kb) for g in range(NKV) for kb in range(nkb)]

                def qk(si):
                    g, kb = steps[si]
                    kw = kws[kb]
                    sbk = si % 2
                    mmg([(ps[(sbk, 5 + sbk)[a]][0:kw, 0:2 * QB], kT[a * 64:(a + 1) * 64, g, kb * 128:kb * 128 + kw],
                          qT[a * 64:(a + 1) * 64, 2 * g:2 * g + 2, q0:q0 + QB], True, True, {}) for a in range(2)],
                        [("kT", g), ("qT", 2 * g), ("qT", 2 * g + 1)], [("ps", sbk), ("ps", 5 + sbk)])

                qk(0)
                for si, (g, kb) in enumerate(steps):
                    ob = 2 + g % 2
                    kw = kws[kb]
                    sbk = si % 2
                    if si + 1 < len(steps):
                        qk(si + 1)
                    for a in range(2):
                        sb_a = (sbk, 5 + sbk)[a]
                        act(ep[sbk][0:kw, a * 2 * QB:(a + 1) * 2 * QB], ps[sb_a][0:kw, 0:2 * QB], AF.Exp, [("ps", sb_a)], [("at_ep", sbk, a)], scale=0.125)
                    tt("dve", pp_[sbk][0:kw, 0:4 * QB].rearrange("p (a t) -> p a t", a=4), ep[sbk][0:kw, 0:4 * QB].rearrange("p (a t) -> p a t", a=4),
                       bc_mid(maskT[0:kw, kb, 0:QB], 4), ALU.mult, [("at_ep", sbk), (mk, kb)], [("at_pp", sbk)])
                    items = []
                    for a in range(2):
                        kw_ = dict(tile_position=(0, a * 64), skip_group_check=True)
                        items.append((ps[ob][a * 64:(a + 1) * 64, 0:2 * QB], Vc[0:kw, kb, g * 64:(g + 1) * 64], pp_[sbk][0:kw, a * 2 * QB:(a + 1) * 2 * QB],
                                      kb == 0, kb == nkb - 1, kw_))
                        items.append((ps[ob][a * 64:(a + 1) * 64, 256:256 + 2 * QB], cbv("ones")[0:kw, 0:64], pp_[sbk][0:kw, a * 2 * QB:(a + 1) * 2 * QB],
                                      False, kb == nkb - 1, kw_))
                    mmg(items, [("Vc", g), "cb", ("at_pp", sbk)], [("ps", ob)])
                    yield
                    if kb == nkb - 1:
                        E("dve", "reciprocal", [("ps", ob)], [("at_rden", g % 2)], out=rden[g % 2][:, 0:2 * QB], in_=ps[ob][:, 256:256 + 2 * QB])
                        tt("dve", qT[:, 2 * g:2 * g + 2, q0:q0 + QB], ps[ob][:, 0:2 * QB].rearrange("p (e t) -> p e t", e=2),
                           rden[g % 2][:, 0:2 * QB].rearrange("p (e t) -> p e t", e=2), ALU.mult, [("ps", ob), ("at_rden", g % 2)],
                           [("qT", 2 * g, qi_), ("qT", 2 * g + 1, qi_)])
                        yield

            def drain2(ga, gb_):
                gens = [g_ for g_ in (ga, gb_) if g_ is not None]
                while gens:
                    for g_ in list(gens):
                        try:
                            next(g_)
                        except StopIteration:
                            gens.remove(g_)

            prev = None
            for qi_ in range(len(qblocks)):
                drain2(mask_stage(qi_), prev)
                prev = attend_stage(qi_)
            drain2(prev, None)

        def ssd_phase(ti, is_p, TW, tok0, segs, SL, sc):
            NSEG = len(segs)
            Lc = 128 if is_p else DS
            xraw = [sb("sd_xraw%d" % i, [128, 4, NSEG, 3 + SL], BF16, sc) for i in range(2)]
            zs = [sb("sd_zs%d" % i, [128, 2, T], BF16, sc) for i in range(2)]
            acc = [sb("sd_acc%d" % i, [128, T], F32, sc) for i in range(2)]
            xbc = [sb("sd_xbc%d" % i, [128, 4, T], BF16, sc) for i in range(2)]
            y1 = [sb("sd_y1%d" % i, [128, 2, T], BF16, sc) for i in range(2)]
            def two(name, shape, dt):
                return [sb("%s%d" % (name, i), shape, dt, sc) for i in range(2)]
            Zt2 = two("sd_Z", [128, 4, 128], F32)
            W22 = two("sd_W2", [128, 4, 128], F32)
            acs2 = two("sd_acs", [128, 4], F32)
            E1b2 = two("sd_E1b", [128, 4, 128], BF16)
            Em2 = two("sd_E", [128, 4, 128], BF16)
            LT2 = two("sd_LT", [128, 4, 128], BF16)
            CpT2 = two("sd_CpT", [128, 4, 128], BF16)
            cbs2 = two("sd_cbs", [128, 128], BF16)
            Xdt2 = two("sd_Xdt", [128, 4, 64], BF16)
            Xw2 = two("sd_Xw", [128, 4, 64], BF16)
            Btok2 = two("sd_Btok", [128, 128], BF16)
            wsm2 = two("sd_ws", [128, 4], F32)
            CDb2 = two("sd_CDb", [128, 4], F32)
            ckc = {"n": 0}
            NSL = 1 if is_p else NS
            hs4 = sb("sd_hs", [128, NSL, 256], F32, sc)
            hsb4 = sb("sd_hsb", [128, NSL, 256], BF16, sc)
            stt_ = sb("sd_sttmp", [128, 256], F32, sc)
            y2 = [sb("sd_y2%d" % i, [128, T], BF16, sc) for i in range(2)]
            sq1 = sb("sd_sq", [128, T], BF16, sc)
            sq = [sq1, sq1]
            rs1 = sb("sd_rs1", [128, T], F32, sc)
            rstd = sb("sd_rstd", [128, T], F32, sc)
            if not is_p:
                scv = sb("sd_sconv", [128, NXB, NS, 3], BF16, sc)
                P.dma("pool", lambda h: h.dma_start(out=scv[:], in_=sconv_d), writes=["sd_sconv"])
            _, pieces = streams["b2"]
            def proj_stage(g):
                gb = g % 2
                wv, wkey = w_get("b2", g)
                xids = [2 * g, 2 * g + 1, XC + g, XC + SG + g]
                for j in range(6):
                    bank = dense_rr["i"] % 2
                    dense_rr["i"] += 1
                    for kc in range(DC):
                        mm(ps[bank][:, 0:TW], wv[:, kc, j * 128:(j + 1) * 128], hT[:, kc, 0:TW], kc == 0, kc == DC - 1, [wkey, "hT"], [("ps", bank)])
                    if j < 4:
                        xc = xids[j]
                        for si_, (s, t0, ln) in enumerate(segs):
                            act(xraw[gb][:, j, si_, 3:3 + ln], ps[bank][:, t0:t0 + ln], AF.Copy, [("ps", bank)], [("sd_xraw", gb, j)])
                            if is_p:
                                E("pool", "tensor_copy", [("chist", xc)], [("sd_xraw", gb, j)], out=xraw[gb][:, j, si_, 0:3], in_=chist[:, xc, :])
                            else:
                                E("pool", "tensor_copy", ["sd_sconv"], [("sd_xraw", gb, j)], out=xraw[gb][:, j, si_, 0:3], in_=scv[:, xc, si_, :])
                        if is_p:
                            E("pool", "tensor_copy", [("sd_xraw", gb, j)], [("chist", xc)], out=chist[:, xc, :], in_=xraw[gb][:, j, 0, SL:SL + 3])
                            if ti == c.NT - 1:
                                E("pool", "tensor_copy", [("sd_xraw", gb, j)], [("cst", xc)], out=cst[:, xc, 0, :], in_=xraw[gb][:, j, 0, SL:SL + 3])
                        else:
                            E("pool", "tensor_copy", [("sd_xraw", gb, j)], [("cst", xc)], out=cst[:, xc, 1:1 + NS, :], in_=xraw[gb][:, j, :, SL:SL + 3])
                        ab = j % 2
                        cw = lambda k: vv("convw", xc * 4 + k, xc * 4 + k + 1)
                        av = acc[ab][:, 0:TW].rearrange("p (s l) -> p s l", s=NSEG)
                        ts("dve", av, xraw[gb][:, j, :, 0:SL], cw(0), vv("convb", xc, xc + 1), ALU.mult, ALU.add, [("sd_xraw", gb, j), "vec"], [("sd_acc", ab)])
                        for k in range(1, 4):
                            stt("dve", av, xraw[gb][:, j, :, k:k + SL], cw(k), av, ALU.mult, ALU.add, [("sd_xraw", gb, j), "vec", ("sd_acc", ab)], [("sd_acc", ab)])
                        act(xbc[gb][:, j, 0:TW], acc[ab][:, 0:TW], AF.Silu, [("sd_acc", ab)], [("sd_xbc", gb, j)])
                    else:
                        act(zs[gb][:, j - 4, 0:TW], ps[bank][:, 0:TW], AF.Silu, [("ps", bank)], [("sd_zs", gb, j - 4)])
                    yield
                if "xbc" in debug and g == 0:
                    dbg("xbc%d" % ti, xbc[gb][:, :, 0:TW], [128, 4, TW], ("sd_xbc", gb))
                yield

            def scan_stage(g):
                gb = g % 2
                if not is_p:
                    DM("sp", hs4[:, :, :], sssm_d[:, g].rearrange("s n f -> n s f"), writes=["sd_hs"])
                for si_, (s, t0s, ln) in enumerate(segs):
                    if is_p:
                        hcur = hst[:, g, :]
                        hkey = ("hst", g)
                    else:
                        hcur = hs4[:, si_, :]
                        hkey = ("sd_hs", si_)
                    hsb = hsb4[:, si_ if not is_p else 0, :]
                    hbk = ("sd_hsb", si_ if not is_p else 0)
                    E("act", "copy", [hkey], [hbk], out=hsb, in_=hcur)
                    for ck in range(ln // Lc):
                        t0 = t0s + ck * Lc
                        blk = t0 // (128 if is_p else DS)
                        pk = ckc["n"] % 2
                        ckc["n"] += 1
                        Zt, W2, acs, E1b, Em, LT, CpT, cbs = Zt2[pk], W22[pk], acs2[pk], E1b2[pk], Em2[pk], LT2[pk], CpT2[pk], cbs2[pk]
                        Xdt, Xw, Btok, wsm, CDb = Xdt2[pk], Xw2[pk], Btok2[pk], wsm2[pk], CDb2[pk]
                        K_ = lambda n: (n, pk)
                        dA = dAt[0:Lc, blk, 4 * g:4 * g + 4]
                        dtv = dtp[0:Lc, blk, 4 * g:4 * g + 4]
                        X0 = xbc[gb][:, 0, t0:t0 + Lc]
                        X1 = xbc[gb][:, 1, t0:t0 + Lc]
                        Bf = xbc[gb][:, 2, t0:t0 + Lc]
                        Cf = xbc[gb][:, 3, t0:t0 + Lc]
                        xk = [("sd_xbc", gb, jj) for jj in range(4)]
                        mm(ps[6][0:Lc, 0:4], cfv("tri")[0:Lc, 0:Lc], dA, True, True, ["cf", "dAt"], [("ps", 6)])
                        act(acs[0:Lc, :], ps[6][0:Lc, 0:4], AF.Copy, [("ps", 6)], [K_("sd_acs")])
                        tt("pool", Zt[0:Lc, :, 0:Lc], bc_last(dA, Lc), bc_mid(cfv("tri")[0:Lc, 0:Lc], 4), ALU.mult, ["dAt", "cf"], [K_("sd_Z")])
                        tt("pool", W2[0:Lc, :, 0:Lc], bc_mid(cfv("mb")[0:Lc, 0:Lc], 4), bc_last(acs[0:Lc, :], Lc), ALU.subtract, ["cf", K_("sd_acs")], [K_("sd_W2")])
                        bk = 4 + pk
                        mm(ps[bk][:, 0:4 * Lc], cfv("ones")[0:Lc, :], Zt[0:Lc, :, 0:Lc], True, False, ["cf", K_("sd_Z")], [("ps", bk)], skip_group_check=True)
                        p4 = ps[bk][:, 0:4 * Lc].rearrange("p (h i) -> p h i", h=4)
                        act(E1b[:, :, 0:Lc], p4, AF.Exp, [("ps", bk)], [K_("sd_E1b")])
                        act(CDb[:, :], p4[:, :, Lc - 1], AF.Exp, [("ps", bk)], [K_("sd_CDb")])
                        mm(ps[bk][0:Lc, 0:4 * Lc], cfv("ident")[0:Lc, 0:Lc], W2[0:Lc, :, 0:Lc], False, True, ["cf", K_("sd_W2")], [("ps", bk)], skip_group_check=True)
                        p5 = ps[bk][0:Lc, 0:4 * Lc].rearrange("p (h i) -> p h i", h=4)
                        act(Em[0:Lc, :, 0:Lc], p5, AF.Exp, [("ps", bk)], [K_("sd_E")])
                        act(wsm[0:Lc, :], p5[:, :, Lc - 1], AF.Exp, [("ps", bk)], [K_("sd_ws")])
                        mm(ps[6][0:Lc, 128:128 + Lc], Bf, Cf, True, True, [xk[2], xk[3]], [("ps", 6)])
                        act(cbs[0:Lc, 0:Lc], ps[6][0:Lc, 128:128 + Lc], AF.Copy, [("ps", 6)], [K_("sd_cbs")])
                        tt("dve", LT[0:Lc, :, 0:Lc], Em[0:Lc, :, 0:Lc], bc_mid(cbs[0:Lc, 0:Lc], 4), ALU.mult, [K_("sd_E"), K_("sd_cbs")], [K_("sd_LT")])
                        tt("dve", CpT[:, :, 0:Lc], E1b[:, :, 0:Lc], bc_mid(Cf, 4), ALU.mult, [K_("sd_E1b"), xk[3]], [K_("sd_CpT")])
                        E("pe", "transpose", [xk[0], "cb"], [("ps", 7)], psb[0:Lc, 0:128], X0, cbv("ident"))
                        E("pe", "transpose", [xk[1], "cb"], [("ps", 7)], psb[0:Lc, 128:256], X1, cbv("ident"))
                        E("pe", "transpose", [xk[2], "cb"], [("ps", 7)], psb[0:Lc, 256:384], Bf, cbv("ident"))
                        tt("dve", Xdt[0:Lc, :, :], psb[0:Lc, 0:256].rearrange("p (h d) -> p h d", h=4), bc_last(dtv, 64), ALU.mult, [("ps", 7), "dtp"], [K_("sd_Xdt")])
                        E("dve", "tensor_copy", [("ps", 7)], [K_("sd_Btok")], out=Btok[0:Lc, :], in_=psb[0:Lc, 256:384])
                        yield
                        items = []
                        for e in range(2):
                            for a in range(2):
                                lh = 2 * e + a
                                kw_ = dict(tile_position=(0, a * 64), skip_group_check=True)
                                items.append((ps[2][a * 64:(a + 1) * 64, e * 128:e * 128 + Lc], Xdt[0:Lc, lh, :], LT[0:Lc, lh, 0:Lc], True, False, kw_))
                                items.append((ps[2][a * 64:(a + 1) * 64, e * 128:e * 128 + Lc], hsb[:, lh * 64:(lh + 1) * 64], CpT[:, lh, 0:Lc], False, True, kw_))
                        mmg(items, [K_("sd_Xdt"), K_("sd_LT"), hbk, K_("sd_CpT")], [("ps", 2)])
                        for e in range(2):
                            XTe = xbc[gb][:, e, t0:t0 + Lc]
                            stt("dve", y1[gb][:, e, t0:t0 + Lc], XTe, vv("dskip", 2 * g + e, 2 * g + e + 1), ps[2][:, e * 128:e * 128 + Lc], ALU.mult, ALU.add,
                                [xk[e], "vec", ("ps", 2)], [("sd_y1", gb, e)])
                        tt("dve", Xw[0:Lc, :, :], Xdt[0:Lc, :, :], bc_last(wsm[0:Lc, :], 64), ALU.mult, [K_("sd_Xdt"), K_("sd_ws")], [K_("sd_Xw")])
                        mm(ps[3][:, 0:256], Btok[0:Lc, :], Xw[0:Lc, :, :], True, True, [K_("sd_Btok"), K_("sd_Xw")], [("ps", 3)])
                        tt("pool", stt_[:, :].rearrange("p (h d) -> p h d", h=4), hcur.rearrange("p (h d) -> p h d", h=4), bc_last(CDb[:, :], 64), ALU.mult,
                           [hkey, K_("sd_CDb")], ["sd_sttmp"])
                        tt("dve", hcur, stt_[:, :], ps[3][:, 0:256], ALU.add, ["sd_sttmp", ("ps", 3)], [hkey])
                        E("act", "copy", [hkey], [hbk], out=hsb, in_=hcur)
                        yield
                    if is_p and ti == c.NT - 1:
                        DM("sp", o_ssm[0, g], hst[:, g, :], reads=[("hst", g)], is_output=True)
                if not is_p:
                    DM("sp", o_ssm[1:1 + NS, g].rearrange("s n f -> n s f"), hs4[:, :, :], reads=["sd_hs"], is_output=True)
                for e in range(2):
                    tt("dve", y2[e][:, 0:TW], y1[gb][:, e, 0:TW], zs[gb][:, e, 0:TW], ALU.mult, [("sd_y1", gb, e), ("sd_zs", gb, e)], [("sd_y2", e)])
                    act(sq[e][:, 0:TW], y2[e][:, 0:TW], AF.Square, [("sd_y2", e)], ["sd_sq"])
                    mm(ps[3][:, 0:TW], cbv("ones"), sq[e][:, 0:TW], e == 0, e == 1, ["sd_sq", "cb"], [("ps", 3)])
                act(rs1[:, 0:TW], ps[3][:, 0:TW], AF.Ln, [("ps", 3)], ["sd_rs1"], scale=1.0 / 256, bias=epsc[:, 0:1])
                act(rstd[:, 0:TW], rs1[:, 0:TW], AF.Exp, ["sd_rs1"], ["sd_rstd"], scale=-0.5)
                for e in range(2):
                    stt("dve", y3T[:, 2 * g + e, 0:TW], y2[e][:, 0:TW], vv("gssm", 2 * g + e, 2 * g + e + 1), rstd[:, 0:TW], ALU.mult, ALU.mult,
                        [("sd_y2", e), "vec", "sd_rstd"], [("y3T", 2 * g + e)])

                yield

            def drain2(ga, gb_):
                gens = [g_ for g_ in (ga, gb_) if g_ is not None]
                while gens:
                    for g_ in list(gens):
                        try:
                            next(g_)
                        except StopIteration:
                            gens.remove(g_)

            drain2(proj_stage(0), None)
            for g in range(SG):
                drain2(scan_stage(g), proj_stage(g + 1) if g + 1 < SG else None)

        if limit is None or limit > 0:
            import os as _os
            tiles = [int(x) for x in _os.environ["TILES"].split(",")] if "TILES" in _os.environ else list(range(c.NT + 1))
            for ti in tiles:
                if ti != tiles[0] or ti == 0:
                    pass
                if do_tile(ti):
                    break
        P.dma("sp", lambda h: h.dma_start(out=o_conv, in_=cst[:]), reads=["cst"], is_output=True)
        P.barrier()
        P.flush(final=True)
    return nc, P


def _chunked(v, n):
    return np.ascontiguousarray(np.asarray(v, np.float32).reshape(n, 128).T)


def _stream(W, pieces, KC):
    mx = max(len(p) for p in pieces) * 128 * KC
    out = np.zeros((len(pieces), 128, mx), np.float32)
    for pi, cols in enumerate(pieces):
        idx = np.concatenate(cols)
        blk = W[:, idx].reshape(KC, 128, len(idx)).transpose(1, 0, 2).reshape(128, KC * len(idx))
        out[pi, :, :blk.shape[1]] = blk
    return out


def host_consts(c):
    cf = np.zeros((128, c.NF), np.float32)
    k = np.arange(128)
    def put(name, a):
        o, w = c.cf_off[name]
        cf[:, o:o + a.shape[1]] = a
    put("tri", (k[:, None] <= k[None, :]).astype(np.float32))
    put("ones", np.ones((128, 128), np.float32))
    put("ident", np.eye(128, dtype=np.float32))
    put("mb", np.where(k[:, None] > k[None, :], -1.0e5, 0.0).astype(np.float32))
    put("mb2", np.where((k[:, None] < 64) & (k[None, :] >= 64), -1.0e30, 0.0).astype(np.float32))
    p2 = np.zeros((128, 32), np.float32)
    for i in range(1, 32):
        p2[:, i] = 2.0 ** -(i - 1)
    put("pow2", p2)
    cb = np.zeros((128, c.NB), np.float32)
    def putb(name, a):
        o, w = c.cb_off[name]
        cb[:, o:o + w] = a
    putb("ident", np.eye(128, dtype=np.float32))
    putb("ones", np.ones((128, 128), np.float32))
    putb("blk", ((k[:, None] // 64) == (k[None, :] // 64)).astype(np.float32))
    rot = np.zeros((128, 128), np.float32)
    for m in range(128):
        if (m % 64) < 32:
            rot[m + 32, m] = -1.0
        else:
            rot[m - 32, m] = 1.0
    putb("rot", rot)
    return cf, cb


def host_weights(c, inp):
    st = c.streams()
    D, DC = c.D, c.DC
    w_in = np.asarray(inp["w_in"][0], np.float32)
    sizes = [c.NH * 64, c.NKV * 64, c.NKV * 64, c.IH * 64, 64, c.IH, c.DI, c.DI + 2 * c.SG * 128, c.SH, 2 * D]
    offs = np.concatenate([[0], np.cumsum(sizes)])
    oq, ok, ov, oqi, oki, owi, oz, oxbc, odt, og = offs[:10]
    ar = np.arange
    out = {}
    ch = lambda base, i: base + i * 128 + ar(128)
    b1 = [ch(oq, i) for i in range(c.QC)]
    b1 += [np.concatenate([ok + g * 64 + ar(64)] * 2) for g in range(c.NKV)]
    b1 += [ch(oqi, i) for i in range(c.IC)]
    b1 += [np.concatenate([oki + ar(64)] * 2)]
    out["w_b1"] = _stream(w_in, [[b1[i] for i in p] for p in st["b1"][1]], DC)
    b2 = []
    for g in range(c.SG):
        b2.append([ch(oxbc, 2 * g), ch(oxbc, 2 * g + 1), ch(oxbc, c.XC + g), ch(oxbc, c.XC + c.SG + g), ch(oz, 2 * g), ch(oz, 2 * g + 1)])
    out["w_b2"] = _stream(w_in, b2, DC)
    gate = [ch(og, i) for i in range(2 * DC)]
    out["w_gate"] = _stream(w_in, [[gate[i] for i in p] for p in st["gate"][1]], DC)
    tokc = np.concatenate([ov + ar(c.NKV * 64), owi + ar(c.IH), odt + ar(c.SH)])
    out["w_tok"] = np.ascontiguousarray(w_in[:, tokc].reshape(DC, 128, c.NTOK).transpose(1, 0, 2).reshape(128, DC * c.NTOK))
    def plain(W, name, KC):
        cols = [i * 128 + ar(128) for i in range(W.shape[1] // 128)]
        return _stream(np.asarray(W, np.float32), [[cols[i] for i in p] for p in st[name][1]], KC)
    out["w_ada"] = plain(inp["w_ada"][0], "ada", DC)
    out["w_ba"] = plain(inp["w_branch_attn"][0], "ba", c.QC)
    out["w_bs"] = plain(inp["w_branch_ssm"][0], "bs", c.XC)
    out["w_out"] = plain(inp["w_out"][0], "out", DC)
    wgu = np.asarray(inp["w_gate_up"][0], np.float32)
    gu = []
    for j in range(c.FC):
        gu.append(j * 128 + ar(128))
        gu.append(c.FF + j * 128 + ar(128))
    out["w_gu"] = _stream(wgu, [[gu[i] for i in p] for p in st["gu"][1]], DC)
    out["w_dn"] = plain(inp["w_down"][0], "dn", c.FC)
    vec = np.zeros((128, c.NV), np.float32)
    def put(name, a):
        o, w = c.vec_off[name]
        assert a.shape == (128, w), (name, a.shape, w)
        vec[:, o:o + w] = a
    put("gmix", _chunked(inp["g_norm_mix"][0], DC))
    put("gffn", _chunked(inp["g_norm_ffn"][0], DC))
    put("bada", _chunked(inp["b_ada"][0], 6 * DC))
    p64 = ar(128) % 64
    put("gq", np.asarray(inp["g_q"][0], np.float32)[p64][:, None])
    put("gk", np.asarray(inp["g_k"][0], np.float32)[p64][:, None])
    wc = np.asarray(inp["w_conv"][0], np.float32)
    put("convw", np.ascontiguousarray(wc.T.reshape(c.NXB, 128, 4).transpose(1, 0, 2).reshape(128, c.NXB * 4)))
    put("convb", _chunked(inp["b_conv"][0], c.NXB))
    hd = (ar(c.XC)[None, :] * 2 + (ar(128) // 64)[:, None])
    put("dskip", np.asarray(inp["d_skip"][0], np.float32)[hd])
    put("gssm", _chunked(inp["g_ssm_norm"][0], c.XC))
    put("dtb", np.broadcast_to(np.asarray(inp["dt_bias"][0], np.float32)[None, :], (128, c.SH)).copy())
    put("alog", np.broadcast_to(np.asarray(inp["a_log"][0], np.float32)[None, :], (128, c.SH)).copy())
    out["vecs"] = vec
    cf, cb = host_consts(c)
    out["cf32"] = cf
    out["cbf"] = cb
    return out


def host_core_inputs(c, inp, r, shared):
    m = dict(shared)
    NS, DS, PAST, SEQ, D, DC = c.NS, c.DS, c.PAST, c.SEQ, c.D, c.DC
    xs = np.concatenate([np.asarray(inp["x_prompt"][r], np.float32)] +
                        [np.asarray(inp["x_sample"][NS * r + s], np.float32) for s in range(NS)], axis=0)
    m["xT"] = np.ascontiguousarray(xs.reshape(c.TT, DC, 128).transpose(2, 1, 0))
    cs = np.concatenate([np.asarray(inp["c_prompt"][r:r + 1], np.float32), np.asarray(inp["c_sample"][NS * r:NS * r + NS], np.float32)], axis=0)
    m["cT"] = np.ascontiguousarray(cs.reshape(c.NSEQ, DC, 128).transpose(2, 1, 0))
    pos = np.concatenate([np.arange(SEQ)] + [PAST + np.arange(DS)] * NS).astype(np.float32)
    inv = (c.THETA ** (-np.arange(32, dtype=np.float32) / 32)).astype(np.float32)
    ang = pos[None, :] * inv[np.arange(128) % 32][:, None]
    m["rope"] = np.ascontiguousarray(np.stack([np.cos(ang), np.sin(ang)], axis=1).astype(np.float32))
    sl = slice(NS * r, NS * r + NS)
    ck = np.asarray(inp["cache_k"][0][sl], np.float32)
    m["ckT"] = np.ascontiguousarray(ck.transpose(0, 3, 2, 1))
    m["ckiT"] = np.ascontiguousarray(np.asarray(inp["cache_ki"][0][sl], np.float32).transpose(0, 2, 1))
    cv = np.asarray(inp["cache_v"][0][sl], np.float32).reshape(NS, PAST // 128, 128, c.NKV * 64)
    m["cv"] = np.ascontiguousarray(cv.transpose(0, 2, 1, 3))
    scv = np.asarray(inp["state_conv"][0][sl], np.float32)
    m["sconv"] = np.ascontiguousarray(scv.reshape(NS, 3, c.NXB, 128).transpose(3, 2, 0, 1))
    ss = np.asarray(inp["state_ssm"][0][sl], np.float32)
    m["sssm"] = np.ascontiguousarray(ss.reshape(NS, c.SG, 4, 64, 128).transpose(0, 1, 4, 2, 3).reshape(NS, c.SG, 128, 256))
    return m


def host_assemble(c, results, nb):
    NS, DS, SEQ, D, DC = c.NS, c.DS, c.SEQ, c.D, c.DC
    CH = c.NXB * 128
    yp, ys, kp, vp, kip, cp, sp_, ks, vs, kis, cs_, ss = ([] for _ in range(12))
    for r in range(nb):
        o = results[r]
        y = np.asarray(o["o_y"]).transpose(2, 1, 0).reshape(c.TT, D)
        k = np.asarray(o["o_k"]).transpose(2, 1, 0)
        ki = np.asarray(o["o_ki"]).T
        v = np.asarray(o["o_v"]).reshape(c.TT, c.NKV, 64)
        cv = np.asarray(o["o_conv"]).transpose(2, 3, 1, 0).reshape(c.NSEQ, 3, CH)
        sm = np.asarray(o["o_ssm"]).reshape(c.NSEQ, c.SG, 128, 4, 64).transpose(0, 1, 3, 4, 2).reshape(c.NSEQ, c.SH, 64, 128)
        yp.append(y[:SEQ]); kp.append(k[:SEQ]); vp.append(v[:SEQ]); kip.append(ki[:SEQ]); cp.append(cv[0]); sp_.append(sm[0])
        for s in range(NS):
            a, b = SEQ + s * DS, SEQ + (s + 1) * DS
            ys.append(y[a:b]); ks.append(k[a:b]); vs.append(v[a:b]); kis.append(ki[a:b]); cs_.append(cv[1 + s]); ss.append(sm[1 + s])
    f = lambda l, lead: np.ascontiguousarray(np.stack(l).astype(np.float32))[None] if lead else np.ascontiguousarray(np.stack(l).astype(np.float32))
    return (f(yp, 0), f(ys, 0), f(kp, 1), f(vp, 1), f(kip, 1), f(cp, 1), f(sp_, 1), f(ks, 1), f(vs, 1), f(kis, 1), f(cs_, 1), f(ss, 1))


_CACHE = {}


def kernel(**inputs):
    c = FULL
    nb = 8
    if "nc" not in _CACHE:
        _CACHE["nc"] = build(c)[0]
    nc = _CACHE["nc"]
    shared = host_weights(c, inputs)
    in_maps = [host_core_inputs(c, inputs, r, shared) for r in range(nb)]
    res = run_bass_kernel_spmd(nc, in_maps, core_ids=list(range(nb)))
    return host_assemble(c, res.results, nb)
```
